# Optimizing a Trainium2 kernel written in Bass

```python
import math, functools
import jax, jax.numpy as jnp
from jax import lax
import numpy as np

D_MODEL = 2048
BATCH = 2
SEQ = 4096
DEPTH = 1

CHUNK = 64
D_LRU = 1024
LRU_HEADS = 16
LRU_HEAD_DIM = D_LRU // LRU_HEADS
LRU_CONV = 4
LRU_C = 8.0
D_SC = 1024
SC_CONV = 3
D_FF = 5632
FFN_CONV = 3
N_BRANCH = 2
EPS = 1e-6
IN_COLS = 2 * D_LRU + 3 * D_SC + N_BRANCH * D_MODEL

kernel_name = "hybrid_rglru_shortconv_convffn_block"


def rmsnorm(x, g):
    xf = x.astype(jnp.float32)
    y = xf * lax.rsqrt(jnp.mean(xf * xf, axis=-1, keepdims=True) + EPS)
    return (y * g.astype(jnp.float32)).astype(x.dtype)


def causal_dwconv(x, w):
    k_w = w.shape[0]
    s = x.shape[1]
    xp = jnp.pad(x, ((0, 0), (k_w - 1, 0), (0, 0)))
    y = xp[:, 0:s] * w[0]
    for k in range(1, k_w):
        y = y + xp[:, k:k + s] * w[k]
    return y


def _lin_combine(left, right):
    a1, b1 = left
    a2, b2 = right
    return a1 * a2, a2 * b1 + b2


def rg_lru(x, w_a, b_a, w_x, b_x, lam):
    bsz, s, d = x.shape
    xf = x.astype(jnp.float32)
    xh = xf.reshape(bsz, s, LRU_HEADS, LRU_HEAD_DIM)
    r = jax.nn.sigmoid(jnp.einsum('bshi,hij->bshj', xh, w_a.astype(jnp.float32)).reshape(bsz, s, d) + b_a.astype(jnp.float32))
    i = jax.nn.sigmoid(jnp.einsum('bshi,hij->bshj', xh, w_x.astype(jnp.float32)).reshape(bsz, s, d) + b_x.astype(jnp.float32))
    log_a = -LRU_C * jax.nn.softplus(-lam.astype(jnp.float32)) * r
    a = jnp.exp(log_a)
    u = jnp.sqrt(-jnp.expm1(2.0 * log_a)) * (i * xf)
    n_chunks = s // CHUNK
    a_c = a.reshape(bsz, n_chunks, CHUNK, d).transpose(1, 0, 2, 3)
    u_c = u.reshape(bsz, n_chunks, CHUNK, d).transpose(1, 0, 2, 3)

    def step(h0, inp):
        ac, uc = inp
        a_cum, b_cum = lax.associative_scan(_lin_combine, (ac, uc), axis=1)
        h = a_cum * h0[:, None, :] + b_cum
        return h[:, -1], h

    h_init = jnp.zeros((bsz, d), jnp.float32)
    _, hs = lax.scan(step, h_init, (a_c, u_c))
    return hs.transpose(1, 0, 2, 3).reshape(bsz, s, d).astype(x.dtype)


def setup_inputs(seed: int = 0) -> dict:
    key = jax.random.key(seed)
    ks = jax.random.split(key, 20)
    f32 = jnp.float32
    nrm = lambda k, shp, fan: jax.random.normal(k, shp, f32) * (fan ** -0.5)
    a0 = jax.random.uniform(ks[9], (D_LRU,), f32, 0.9, 0.999)
    return {
        "x": jax.random.normal(ks[0], (BATCH, SEQ, D_MODEL), f32),
        "g_mix": 1.0 + 0.02 * jax.random.normal(ks[1], (D_MODEL,), f32),
        "w_in": nrm(ks[2], (D_MODEL, IN_COLS), D_MODEL),
        "lru_conv_w": nrm(ks[3], (LRU_CONV, D_LRU), LRU_CONV),
        "lru_conv_b": 0.01 * jax.random.normal(ks[4], (D_LRU,), f32),
        "lru_wa": nrm(ks[5], (LRU_HEADS, LRU_HEAD_DIM, LRU_HEAD_DIM), LRU_HEAD_DIM),
        "lru_ba": 0.01 * jax.random.normal(ks[6], (D_LRU,), f32),
        "lru_wx": nrm(ks[7], (LRU_HEADS, LRU_HEAD_DIM, LRU_HEAD_DIM), LRU_HEAD_DIM),
        "lru_bx": 0.01 * jax.random.normal(ks[8], (D_LRU,), f32),
        "lru_lambda": jnp.log(a0) - jnp.log1p(-a0),
        "lru_w_out": nrm(ks[10], (D_LRU, D_MODEL), D_LRU),
        "sc_conv_w": nrm(ks[11], (SC_CONV, D_SC), SC_CONV),
        "sc_w_out": nrm(ks[12], (D_SC, D_MODEL), D_SC),
        "w_o": nrm(ks[13], (D_MODEL, D_MODEL), D_MODEL),
        "g_ffn": 1.0 + 0.02 * jax.random.normal(ks[14], (D_MODEL,), f32),
        "ffn_w_up": nrm(ks[15], (D_MODEL, 2 * D_FF), D_MODEL),
        "ffn_conv_w": nrm(ks[16], (FFN_CONV, 2 * D_FF), FFN_CONV),
        "ffn_w_down": nrm(ks[17], (D_FF, D_MODEL), D_FF),
        "g_final": 1.0 + 0.02 * jax.random.normal(ks[18], (D_MODEL,), f32),
    }


def reference(x, g_mix, w_in, lru_conv_w, lru_conv_b, lru_wa, lru_ba, lru_wx, lru_bx,
              lru_lambda, lru_w_out, sc_conv_w, sc_w_out, w_o, g_ffn, ffn_w_up,
              ffn_conv_w, ffn_w_down, g_final):
    for _ in range(DEPTH):
        h = rmsnorm(x, g_mix)
        p = h @ w_in
        o = 0
        lru_x = p[..., o:o + D_LRU]; o += D_LRU
        lru_gate = p[..., o:o + D_LRU]; o += D_LRU
        sc_b = p[..., o:o + D_SC]; o += D_SC
        sc_c = p[..., o:o + D_SC]; o += D_SC
        sc_v = p[..., o:o + D_SC]; o += D_SC
        gate_lru = p[..., o:o + D_MODEL]; o += D_MODEL
        gate_sc = p[..., o:o + D_MODEL]

        xc = causal_dwconv(lru_x, lru_conv_w) + lru_conv_b
        y_lru = rg_lru(xc, lru_wa, lru_ba, lru_wx, lru_bx, lru_lambda)
        y_lru = (jax.nn.gelu(lru_gate) * y_lru) @ lru_w_out

        y_sc = (sc_b * causal_dwconv(sc_c * sc_v, sc_conv_w)) @ sc_w_out

        merged = jax.nn.sigmoid(gate_lru) * y_lru + jax.nn.sigmoid(gate_sc) * y_sc
        x = x + merged @ w_o

        h = rmsnorm(x, g_ffn)
        u = causal_dwconv(h @ ffn_w_up, ffn_conv_w)
        ff_gate, ff_val = u[..., :D_FF], u[..., D_FF:]
        x = x + (jax.nn.silu(ff_gate) * ff_val) @ ffn_w_down
    return rmsnorm(x, g_final)
```

```python
import numpy as np
from contextlib import ExitStack
import concourse.bass as bass
import concourse.mybir as mybir
from concourse.bass_utils import run_bass_kernel_spmd

F32 = mybir.dt.float32
BF16 = mybir.dt.bfloat16
U8 = mybir.dt.uint8
AF = mybir.ActivationFunctionType
ALU = mybir.AluOpType

D = 2048
KC = 16
W = 1032
NG = 3
GW = 344
NT = 8
HALO = 8
DFF = 5632
NFC = 44
FG = 4
NGRP = 11
EPS = 1e-6
NSLOT = 5
SLAB = 8192

P_CW4, P_CB, P_BA, P_BX, P_LAM, P_SCW, P_MASK, P_FCW = 0, 32, 40, 48, 56, 64, 88, 92
NPAR = 92 + 88 * 3

DEBUG = False


class Buf:
    def __init__(self, name):
        self.name = name
        self.prev = []
        self.ws = []
        self.rs = []
        self.dsem = None
        self.dcnt = 0

    def all_tokens(self):
        return list(self.prev) + list(self.ws) + list(self.rs)


class Sched:
    def __init__(self, nc, es):
        self.nc = nc
        self.es = es
        self.names = ["pe", "act", "dve", "pool", "sp"]
        self.sem = {k: es.enter_context(nc.semaphore("s_" + k)) for k in self.names}
        self.cnt = {k: 0 for k in self.names}
        self.seen = {k: {} for k in self.names}
        self.prog = {k: [] for k in self.names}
        self.nsem = 0

    def _deps(self, reads, writes, uwrites, extra):
        toks = []
        for b in reads:
            toks += b.prev + b.ws
        for b in writes:
            toks += b.prev + b.ws + b.rs
        for b in uwrites:
            toks += b.prev
        toks += list(extra)
        return toks

    def _emit_waits(self, e, toks):
        need = {}
        for t in toks:
            if t is None:
                continue
            sem, val = t
            key = sem.name
            if self.seen[e].get(key, 0) >= val:
                continue
            if key not in need or need[key][1] < val:
                need[key] = (sem, val)
        for key, (sem, val) in need.items():
            self.prog[e].append(("wait", sem, val))
            self.seen[e][key] = val

    def _post(self, tok, reads, writes, uwrites):
        for b in reads:
            b.rs.append(tok)
        for b in writes:
            b.prev = []
            b.ws = [tok]
            b.rs = []
        for b in uwrites:
            b.ws.append(tok)

    def op(self, e, fn, reads=(), writes=(), uwrites=(), extra=()):
        toks = self._deps(reads, writes, uwrites, extra)
        self._emit_waits(e, toks)
        self.cnt[e] += 1
        tok = (self.sem[e], self.cnt[e])
        self.prog[e].append(("op", fn, self.sem[e], 1))
        self._post(tok, reads, writes, uwrites)
        return tok

    def dma(self, e, fn, dbuf, reads=(), writes=(), uwrites=(), extra=()):
        if dbuf.dsem is None:
            dbuf.dsem = self.es.enter_context(self.nc.semaphore("d_" + dbuf.name))
            self.nsem += 1
        toks = self._deps(reads, writes, uwrites, extra)
        self._emit_waits(e, toks)
        dbuf.dcnt += 16
        tok = (dbuf.dsem, dbuf.dcnt)
        self.prog[e].append(("op", fn, dbuf.dsem, 16))
        self._post(tok, reads, writes, uwrites)
        return tok

    def wait_only(self, e, toks):
        self._emit_waits(e, toks)

    def epoch(self, b):
        b.prev = b.all_tokens()
        b.ws = []
        b.rs = []

    def takeover(self, new, olds):
        toks = list(new.all_tokens())
        for o in olds:
            toks += o.all_tokens()
        new.prev = toks
        new.ws = []
        new.rs = []

    def replay(self, block):
        nc = self.nc

        def mk(name):
            items = self.prog[name]

            def body(eng):
                for it in items:
                    if it[0] == "wait":
                        eng.wait_ge(it[1], it[2])
                    else:
                        ins = it[1](eng)
                        ins.then_inc(it[2], it[3])
            return body

        block.tensor(mk("pe"))
        block.scalar(mk("act"))
        block.vector(mk("dve"))
        block.gpsimd(mk("pool"))
        block.sync(mk("sp"))


def build_program():
    nc = bass.Bass("TRN2", target_bir_lowering=False)
    xw = nc.dram_tensor("xw", [4 * 1024 + HALO, D], F32, kind="ExternalInput").ap()
    w_lx = nc.dram_tensor("w_lx", [4, 128, 4096], F32, kind="ExternalInput").ap()
    w_inp = nc.dram_tensor("w_inp", [36, 128, 4096], F32, kind="ExternalInput").ap()
    w_oc = nc.dram_tensor("w_oc", [8, 128, 4096], F32, kind="ExternalInput").ap()
    w_o = nc.dram_tensor("w_o", [8, 128, 4096], F32, kind="ExternalInput").ap()
    w_up = nc.dram_tensor("w_up", [44, 128, 4096], F32, kind="ExternalInput").ap()
    w_dn = nc.dram_tensor("w_dn", [22, 128, 4096], F32, kind="ExternalInput").ap()
    gwa_d = nc.dram_tensor("gwa", [128, 8 * 128], F32, kind="ExternalInput").ap()
    gwx_d = nc.dram_tensor("gwx", [128, 8 * 128], F32, kind="ExternalInput").ap()
    par_d = nc.dram_tensor("par", [128, NPAR], F32, kind="ExternalInput").ap()
    g3_d = nc.dram_tensor("g3", [3, 128, D], F32, kind="ExternalInput").ap()
    id_d = nc.dram_tensor("ident", [128, 128], F32, kind="ExternalInput").ap()
    out_d = nc.dram_tensor("out", [1024, D], F32, kind="ExternalOutput").ap()
    dbg = {}
    if DEBUG:
        dbg["hT"] = nc.dram_tensor("dbg_hT", [128, KC * W], BF16, kind="ExternalOutput").ap()
        dbg["hs"] = nc.dram_tensor("dbg_hs", [128, 8 * W], F32, kind="ExternalOutput").ap()
        dbg["z"] = nc.dram_tensor("dbg_z", [128, 8 * W], BF16, kind="ExternalOutput").ap()
        dbg["ylg"] = nc.dram_tensor("dbg_ylg", [128, 8 * W], BF16, kind="ExternalOutput").ap()
        dbg["merged"] = nc.dram_tensor("dbg_merged", [128, KC * W], BF16, kind="ExternalOutput").ap()
        dbg["xr"] = nc.dram_tensor("dbg_xr", [128, NT * D], F32, kind="ExternalOutput").ap()
        dbg["h2T"] = nc.dram_tensor("dbg_h2T", [128, KC * W], BF16, kind="ExternalOutput").ap()

    with ExitStack() as es:
        base0 = (nc.sbuf_base + 31) // 32 * 32
        ARENA = 207872
        arena = nc.alloc_sbuf_tensor("arena", [128, ARENA], U8)
        cur = [base0]

        def at(name, shape, dt, off):
            return nc.alloc_sbuf_tensor_at(name, shape, dt, offset=off)

        R_A = base0
        R_B = R_A + 33024
        R_C = R_B + 33024
        R_D = R_C + 33024
        D_SIZE = 50176
        R_E = R_D + D_SIZE
        R_P = R_E + NSLOT * SLAB
        assert R_P + 15872 <= base0 + ARENA, (R_P + 15872, base0 + ARENA)

        hT = [at("hT0", [128, KC, W], BF16, R_A), at("hT1", [128, KC, W], BF16, R_B)]
        zT = at("zT", [128, 8, W], BF16, R_A)
        ylgT = at("ylgT", [128, 8, W], BF16, R_A + 16512)
        xr = at("xr", [128, NT, D], F32, R_A)
        xt = [at("xt0", [128, D], F32, R_C), at("xt1", [128, D], F32, R_C + 8192)]
        ht = [at("ht0", [128, D], BF16, R_C + 16384), at("ht1", [128, D], BF16, R_C + 20480)]
        merged = at("merged", [128, KC, W], BF16, R_C)
        h2T = at("h2T", [128, KC, W], BF16, R_C)
        o = R_D
        lx = [at("lx0", [128, W], F32, o), at("lx1", [128, W], F32, o + 4128)]
        xc = at("xc", [128, W], F32, o + 2 * 4128)
        rr = at("rr", [128, W], F32, o + 3 * 4128)
        xc2 = [xc, at("xc_b", [128, W], F32, o + 9 * 4128 + 2080)]
        rr2 = [rr, at("rr_b", [128, W], F32, o + 10 * 4128 + 2080)]
        xcb2 = [at("xcb", [128, W], BF16, o + 9 * 4128), at("xcb_b", [128, W], BF16, o + 11 * 4128 + 2080)]
        assert 11 * 4128 + 2080 + 2064 <= D_SIZE
        ii = at("ii", [128, W], F32, o + 4 * 4128)
        aa = at("aa", [128, W], F32, o + 5 * 4128)
        hs = [at("hs0", [128, W], F32, o + 6 * 4128), at("hs1", [128, W], F32, o + 7 * 4128)]
        gg = at("gg", [128, W], F32, o + 8 * 4128)
        ii2 = [ii, at("ii_b", [128, W], F32, R_C + 24576)]
        aa2 = [aa, at("aa_b", [128, W], F32, R_C + 24576 + 4128)]
        assert 24576 + 2 * 4128 <= 33024
        assert 9 * 4128 + 2064 <= D_SIZE
        sb_b = at("sb_b", [128, W], F32, o)
        sb_c = at("sb_c", [128, W], F32, o + 4128)
        sb_cv = at("sb_cv", [128, W], F32, o + 2 * 4128)
        sb_acc = at("sb_acc", [128, W], F32, o + 3 * 4128)
        sgl = at("sgl", [128, W], F32, o + 4 * 4128)
        tm1 = at("tm1", [128, W], F32, o + 5 * 4128)
        xrh = at("xrh", [128, D], F32, o)
        h2t = [at("h2t0", [128, D], BF16, o + 8192), at("h2t1", [128, D], BF16, o + 12288)]
        sqj = at("sqj", [128, D], BF16, o + 16384)
        gated = [at("gated0", [128, FG, 1024], BF16, o + 20736), at("gated1", [128, FG, 1024], BF16, o + 28928)]
        upg = at("upg", [128, W], F32, o)
        upv = at("upv", [128, W], F32, o + 4128)
        acg = at("acg", [128, W], F32, o + 2 * 4128)
        acv = at("acv", [128, W], F32, o + 3 * 4128)
        sgf = at("sgf", [128, W], F32, o + 4 * 4128)
        assert 5 * 4128 <= 20736
        ostg = [at("ostg0", [128, D], F32, o + 37120), at("ostg1", [128, D], F32, o)]
        assert 37120 + 8192 <= D_SIZE
        slab = [at("slab%d" % i, [128, SLAB // 2], BF16, R_E + i * SLAB) for i in range(NSLOT)]
        o = R_P
        gB = at("gB", [128, D], F32, o); o += 8192
        par = at("par_s", [128, NPAR], F32, o); o += ((NPAR * 4 + 31) // 32) * 32
        gwa = at("gwa_s", [128, 8, 128], BF16, o); o += 2048
        gwx = at("gwx_s", [128, 8, 128], BF16, o); o += 2048
        ident = at("ident_s", [128, 128], BF16, o); o += 256
        sm = at("small", [128, 128], F32, o); o += 512
        assert o <= R_P + 15872
        coef = sm[:, 0:8]
        eps_ap = sm[:, 72:73]
        coef2 = sm[:, 8:16]
        state = sm[:, 16:24]
        hh = sm[:, 24:40]

        ps = nc.alloc_psum_tensor("ps", [128, 8, 512], F32)

        S = Sched(nc, es)
        B = {}
        def buf(name):
            if name not in B:
                B[name] = Buf(name)
            return B[name]
        bank = [buf("bank%d" % i) for i in range(8)]

        jobs = []
        slab_buf = [buf("slab%d" % i) for i in range(NSLOT)]
        state_w = {"next_issue": 0}

        def add_job(kind, src, a, b_):
            jobs.append((kind, src, a, b_))
            return len(jobs) - 1

        def slab_view(slot, kind):
            s = slab[slot]
            if kind == "K16":
                return s[:, :].rearrange("p (k n) -> p k n", k=16)
            if kind == "K8":
                return s[:, :].rearrange("p (k n) -> p k n", k=8)
            if kind == "DN":
                return s[:, :].rearrange("p (k n) -> p k n", k=2)
            raise ValueError(kind)

        def issue_job(j):
            kind, src, a, b_ = jobs[j]
            slot = j % NSLOT
            view = slab[slot][:, :].rearrange("p (a n) -> p a n", a=2)
            src_ap = src[a, :, :].rearrange("p (a n) -> p a n", a=2)
            S.dma("pool", lambda g, v=view, s_=src_ap: g.dma_start(out=v, in_=s_), slab_buf[slot],
                  writes=[slab_buf[slot]])

        def ensure_issued(upto):
            while state_w["next_issue"] <= min(upto, len(jobs) - 1):
                issue_job(state_w["next_issue"])
                state_w["next_issue"] += 1

        def use_slab(j):
            ensure_issued(j)
            return j % NSLOT

        def release_slab(j):
            ensure_issued(j + NSLOT)

        J_pre = [[add_job("K16", w_lx, s, 0) for s in range(4)] for k in range(3)]
        J_lx = [add_job("K16", w_inp, c, 0) for c in range(8)]
        J_sc = [add_job("K16", w_inp, 8 + s, 0) for s in range(12)]
        J_ig, J_op = [], []
        for jp in range(8):
            J_ig.append(add_job("K16", w_inp, 20 + 2 * jp, 0))
            J_op.append(add_job("K8", w_oc, jp, 0))
            J_ig.append(add_job("K16", w_inp, 20 + 2 * jp + 1, 0))
        J_wo = [add_job("K16", w_o, s, 0) for s in range(8)]
        J_up, J_dn = [], []
        for G in range(NGRP):
            J_up.append([add_job("K16", w_up, G * FG + c, 0) for c in range(FG)])
            if G >= 1:
                J_dn.append([add_job("DN", w_dn, (G - 1) * 2 + h, 0) for h in range(2)])
        J_dn.append([add_job("DN", w_dn, (NGRP - 1) * 2 + h, 0) for h in range(2)])

        b_par, b_gB, b_gw, b_id, b_sm = buf("par"), buf("gB"), buf("gw"), buf("ident"), buf("sm")
        S.dma("sp", lambda q: q.dma_start(out=par[:, :], in_=par_d[:, :]), b_par, writes=[b_par])
        S.dma("sp", lambda q: q.dma_start(out=gB[:, :], in_=g3_d[0, :, :]), b_gB, writes=[b_gB])
        S.dma("pool", lambda q: q.dma_start(out=ident[:, :], in_=id_d[:, :]), b_id, writes=[b_id])
        S.dma("pool", lambda q: q.dma_start(out=gwa[:, :, :].rearrange("p k n -> p (k n)"), in_=gwa_d[:, :]), b_gw, writes=[b_gw])
        S.dma("pool", lambda q: q.dma_start(out=gwx[:, :, :].rearrange("p k n -> p (k n)"), in_=gwx_d[:, :]), b_gw, uwrites=[b_gw])
        ensure_issued(NSLOT - 1)
        S.op("act", lambda a: a.activation(out=sm[:, 48:56], in_=par[:, P_LAM:P_LAM + 8], func=AF.Exp, scale=-1.0),
             reads=[b_par], writes=[b_sm])
        S.op("act", lambda a: a.activation(out=sm[:, 48:56], in_=sm[:, 48:56], func=AF.Ln, bias=1.0, scale=1.0),
             writes=[b_sm])
        S.op("dve", lambda v: v.tensor_scalar(out=coef, in0=sm[:, 48:56], scalar1=-8.0, scalar2=None, op0=ALU.mult),
             writes=[b_sm])
        S.op("dve", lambda v: v.tensor_scalar(out=coef2, in0=sm[:, 48:56], scalar1=-4.0, scalar2=None, op0=ALU.mult),
             writes=[b_sm])
        S.op("dve", lambda v: v.tensor_scalar(out=sm[:, 56:64], in0=par[:, P_BA:P_BA + 8], scalar1=0.5, scalar2=None, op0=ALU.mult),
             reads=[b_par], writes=[b_sm])
        S.op("dve", lambda v: v.tensor_scalar(out=sm[:, 64:72], in0=par[:, P_BX:P_BX + 8], scalar1=0.5, scalar2=None, op0=ALU.mult),
             reads=[b_par], writes=[b_sm])
        S.op("dve", lambda v: v.memset(sm[:, 16:40], 0.0), writes=[b_sm])
        S.op("dve", lambda v: v.memset(sm[:, 72:73], EPS), writes=[b_sm])
        b_xc, b_hs = buf("xc"), [buf("hs0"), buf("hs1")]
        S.op("dve", lambda v: v.memset(xc[:, 0:8], 0.0), writes=[b_xc])
        for s_ in range(2):
            S.op("dve", lambda v, s_=s_: v.memset(hs[s_][:, 0:8], 0.0), writes=[b_hs[s_]])

        rot = {"set": 0, "wo": 0}

        def next_set():
            s_ = rot["set"]
            rot["set"] ^= 1
            return s_

        def mm_result(setid, lhs_fn, rhs_fn, nk, reads, extra=()):
            b0 = 3 * setid
            banks = bank[b0:b0 + 3]

            def fn(t):
                last = None
                for k in range(nk):
                    l_ = lhs_fn(k)
                    for g in range(NG):
                        last = t.matmul(ps[:, b0 + g, 0:GW], lhsT=l_, rhs=rhs_fn(k, g),
                                        start=(k == 0), stop=(k == nk - 1))
                return last
            return S.op("pe", fn, reads=reads, writes=banks, extra=extra), b0, banks

        def ps3(b0):
            return ps[:, b0:b0 + 3, 0:GW]

        def v3(t_ap):
            return t_ap.rearrange("p (g n) -> p g n", g=NG)

        def colp(c0):
            return par[:, c0:c0 + 1]

        b_xt, b_ht = [buf("xt0"), buf("xt1")], [buf("ht0"), buf("ht1")]
        b_hT = [buf("hT0"), buf("hT1")]
        fe_cnt = {"n": 0}

        def rmsnorm_tile(src_tile, ntok, dst_bf, junk_bf, b_src, b_dst, b_junk):
            q = fe_cnt["n"] % 2
            fe_cnt["n"] += 1
            ssc = sm[0:ntok, 40 + q:41 + q]
            ss2 = sm[0:ntok, 42 + q:43 + q]
            rs_ = sm[0:ntok, 44 + q:45 + q]
            bq = buf("ssq%d" % q)
            S.op("act", lambda a: a.activation(out=junk_bf[0:ntok, :], in_=src_tile[0:ntok, :], func=AF.Square, accum_out=ssc),
                 reads=[b_src], writes=[b_junk, bq])
            S.op("act", lambda a: a.activation(out=ss2, in_=ssc, func=AF.Copy), writes=[bq])
            S.op("act", lambda a: a.activation(out=rs_, in_=ss2, func=AF.Sqrt, scale=1.0 / D, bias=sm[0:ntok, 72:73]), writes=[bq])
            S.op("dve", lambda v: v.reciprocal(out=rs_, in_=rs_), writes=[bq])
            S.op("dve", lambda v: v.scalar_tensor_tensor(out=dst_bf[0:ntok, :], in0=src_tile[0:ntok, :], scalar=rs_,
                                                        in1=gB[0:ntok, :], op0=ALU.mult, op1=ALU.mult),
                 reads=[b_src, bq, b_gB], writes=[b_dst])
            return rs_, bq

        def transpose_tile(src_bf, ntok, dstT, col0, b_src, b_dstT, act_half0=True):
            for half in range(2):
                bk = 6 + half
                pb = ps[:, bk, :].bitcast(BF16)

                def fn(t, half=half, pb=pb):
                    last = None
                    for i in range(8):
                        cc = half * 8 + i
                        last = t.transpose(pb[:, i * 128:i * 128 + ntok], src_bf[0:ntok, cc * 128:(cc + 1) * 128],
                                           ident[0:ntok, 0:ntok])
                    return last
                S.op("pe", fn, reads=[b_src, b_id], writes=[bank[bk]])
                src3 = pb.rearrange("p (i n) -> p i n", i=8)[:, :, 0:ntok]
                dst3 = dstT[:, half * 8:half * 8 + 8, col0:col0 + ntok]
                if half == 0 and act_half0:
                    S.op("act", lambda a, s3=src3, d3=dst3: a.activation(out=d3, in_=s3, func=AF.Copy),
                         reads=[bank[bk]], uwrites=[b_dstT])
                else:
                    S.op("dve", lambda v, s3=src3, d3=dst3: v.tensor_copy(out=d3, in_=s3),
                         reads=[bank[bk]], uwrites=[b_dstT])

        def fe_norm(k, t):
            ntok = HALO if t == 0 else 128
            col0 = 0 if t == 0 else HALO + 128 * (t - 1)
            row0 = 1024 * k + col0
            xs = fe_cnt["n"] % 2
            S.dma("sp", lambda q: q.dma_start(out=xt[xs][0:ntok, :], in_=xw[row0:row0 + ntok, :]), b_xt[xs],
                  writes=[b_xt[xs]])
            rmsnorm_tile(xt[xs], ntok, ht[xs], ht[xs], b_xt[xs], b_ht[xs], b_ht[xs])
            return (k, xs, ntok, col0)

        def fe_tr(k, xs, ntok, col0):
            transpose_tile(ht[xs], ntok, hT[k % 2], col0, b_ht[xs], b_hT[k % 2], act_half0=False)

        def fe_tile(k, t):
            fe_tr(*fe_norm(k, t))

        pending_tr = []

        b_lx = [buf("lx0"), buf("lx1")]
        b_rr, b_ii, b_aa, b_xcb, b_gg = buf("rr"), buf("ii"), buf("aa"), buf("xcb"), buf("gg")
        b_xc2 = [b_xc, buf("xc_b")]
        b_rr2 = [b_rr, buf("rr_b")]
        b_xcb2 = [b_xcb, buf("xcb_b")]
        b_ii2 = [b_ii, buf("ii_b")]
        b_aa2 = [b_aa, buf("aa_b")]
        b_ylg, b_z = buf("ylg"), buf("z")
        lxc = {"n": 0, "h": 0}

        st_l = {}

        def lru_part1(k, c, job, colbase):
            hsl = k % 2
            slot = use_slab(job)
            wv = slab_view(slot, "K16")
            ls = lxc["n"] % 2
            lxc["n"] += 1
            q = c % 2
            setid = c % 2
            tok, b0, banks = mm_result(setid, lambda kk: wv[:, kk, colbase:colbase + 128],
                                       lambda kk, g: hT[hsl][:, kk, g * GW:(g + 1) * GW], KC,
                                       reads=[slab_buf[slot], b_hT[hsl]])
            if k < 3 and colbase == 128:
                release_slab(job)
            st_l[(k, c)] = (slot, wv, ls, b0, banks)

        def lru_part1_ew(k, c):
            lru_part1_ew_a(k, c)
            lru_part1_ew_b(k, c)

        def lru_part1_ew_a(k, c):
            slot, wv, ls, b0, banks = st_l[(k, c)]
            S.op("act", lambda a: a.activation(out=v3(lx[ls][:, :]), in_=ps3(b0), func=AF.Copy),
                 reads=banks, writes=[b_lx[ls]])

        def lru_part1_ew_b(k, c):
            slot, wv, ls, b0, banks = st_l[(k, c)]
            q = c % 2
            S.op("act", lambda a: a.activation(out=v3(xc2[q][:, :]), in_=ps3(b0), func=AF.Identity,
                                               scale=colp(P_CW4 + c * 4 + 3), bias=colp(P_CB + c)),
                 reads=banks + [b_par], writes=[b_xc2[q]])
            for tap in (2, 1, 0):
                sh = 3 - tap
                S.op("dve", lambda v, tap=tap, sh=sh: v.scalar_tensor_tensor(
                    out=xc2[q][:, 3:W], in0=lx[ls][:, 3 - sh:W - sh], scalar=colp(P_CW4 + c * 4 + tap), in1=xc2[q][:, 3:W],
                    op0=ALU.mult, op1=ALU.add), reads=[b_lx[ls], b_par], writes=[b_xc2[q]])
            S.op("dve", lambda v: v.tensor_copy(out=xcb2[q][:, :], in_=xc2[q][:, :]), reads=[b_xc2[q]], writes=[b_xcb2[q]])

        def lru_part2a(k, c, which_list=(0, 1)):
            q = c % 2
            rrq, b_rrq = rr2[q], b_rr2[q]
            for which in which_list:
                gw_ = gwa if which == 0 else gwx
                dst = rrq if which == 0 else ii2[q]
                bdst = b_rrq if which == 0 else b_ii2[q]
                pcol = (56 if which == 0 else 64) + c
                def gfn(pe, gw_=gw_):
                    last = None
                    for g in range(2):
                        last = pe.matmul(ps[:, 6 + g, :], lhsT=gw_[:, c, :], rhs=xcb2[q][:, HALO + 512 * g:HALO + 512 * (g + 1)],
                                         start=True, stop=True)
                    return last
                bks = [bank[6], bank[7]]
                S.op("pe", gfn, reads=[b_gw, b_xcb2[q]], writes=bks)
                S.op("act", lambda a, dst=dst, pcol=pcol: a.activation(
                    out=dst[:, HALO:W].rearrange("p (g n) -> p g n", g=2), in_=ps[:, 6:8, :], func=AF.Tanh,
                    bias=sm[:, pcol:pcol + 1], scale=0.5),
                    reads=bks + [b_sm], writes=[bdst])

        def lru_part2b(k, c, job):
            hsl = k % 2
            slot, wv, ls_, b0_, banks_ = st_l.pop((k, c))
            q = c % 2
            xcq, rrq, b_xcq, b_rrq = xc2[q], rr2[q], b_xc2[q], b_rr2[q]
            aq, iq, b_aq, b_iq = aa2[q], ii2[q], b_aa2[q], b_ii2[q]
            S.op("act", lambda a: a.activation(out=aq[:, HALO:W], in_=rrq[:, HALO:W], func=AF.Exp, scale=sm[:, 8 + c:9 + c], bias=sm[:, 8 + c:9 + c]),
                 reads=[b_rrq, b_sm], writes=[b_aq])
            S.op("act", lambda a: a.activation(out=rrq[:, HALO:W], in_=rrq[:, HALO:W], func=AF.Exp, scale=sm[:, c:c + 1], bias=sm[:, c:c + 1]),
                 reads=[b_sm], writes=[b_rrq])
            S.op("act", lambda a: a.activation(out=rrq[:, HALO:W], in_=rrq[:, HALO:W], func=AF.Sqrt, scale=-1.0, bias=1.0),
                 writes=[b_rrq])
            S.op("dve", lambda v: v.scalar_tensor_tensor(out=iq[:, HALO:W], in0=iq[:, HALO:W], scalar=1.0, in1=xcq[:, HALO:W], op0=ALU.add, op1=ALU.mult),
                 reads=[b_xcq], writes=[b_iq])
            S.op("dve", lambda v: v.scalar_tensor_tensor(out=iq[:, HALO:W], in0=iq[:, HALO:W], scalar=0.5, in1=rrq[:, HALO:W], op0=ALU.mult, op1=ALU.mult),
                 reads=[b_rrq], writes=[b_iq])
            hq = lxc["h"] % 2
            lxc["h"] += 1
            S.op("dve", lambda v: v.tensor_tensor_scan(out=hs[hq][:, HALO:W], data0=aq[:, HALO:W], data1=iq[:, HALO:W],
                                                       initial=sm[:, 16 + c:17 + c], op0=ALU.mult, op1=ALU.add),
                 reads=[b_aq, b_iq, b_sm], writes=[b_hs[hq]])
            if k < 3:
                if k == 2:
                    S.op("dve", lambda v: v.tensor_scalar(out=sm[:, 24 + 2 * c:26 + 2 * c], in0=hs[hq][:, W - 2:W],
                                                          scalar1=colp(P_MASK + 2), scalar2=None, op0=ALU.mult),
                         reads=[b_hs[hq], b_par], writes=[b_sm])
                S.op("dve", lambda v: v.tensor_scalar(out=sm[:, 16 + c:17 + c], in0=hs[hq][:, W - 1:W],
                                                      scalar1=colp(P_MASK + k), scalar2=None, op0=ALU.mult),
                     reads=[b_hs[hq], b_par], writes=[b_sm])
                return
            S.op("dve", lambda v: v.tensor_copy(out=hs[hq][:, HALO - 2:HALO], in_=sm[:, 24 + 2 * c:26 + 2 * c]),
                 reads=[b_sm], writes=[b_hs[hq]])
            sid = (c + 1) % 2
            tk, gb0, gbks = mm_result(sid, lambda kk: wv[:, kk, 128:256],
                                      lambda kk, g: hT[hsl][:, kk, g * GW:(g + 1) * GW], KC,
                                      reads=[slab_buf[slot], b_hT[hsl]])
            release_slab(job)
            S.op("act", lambda a: a.activation(out=v3(gg[:, :]), in_=ps3(gb0), func=AF.Copy), reads=gbks, writes=[b_gg])
            S.op("act", lambda a: a.activation(out=aq[:, :], in_=gg[:, :], func=AF.Square),
                 reads=[b_gg], writes=[b_aq])
            S.op("dve", lambda v: v.tensor_scalar(out=aq[:, :], in0=aq[:, :], scalar1=0.044715, scalar2=1.0,
                                                  op0=ALU.mult, op1=ALU.add), writes=[b_aq])
            S.op("dve", lambda v: v.tensor_tensor(out=aq[:, :], in0=aq[:, :], in1=gg[:, :], op=ALU.mult),
                 reads=[b_gg], writes=[b_aq])
            S.op("act", lambda a: a.activation(out=aq[:, :], in_=aq[:, :], func=AF.Tanh, scale=0.7978845608028654),
                 writes=[b_aq])
            S.op("dve", lambda v: v.scalar_tensor_tensor(out=gg[:, :], in0=aq[:, :], scalar=1.0, in1=gg[:, :], op0=ALU.add, op1=ALU.mult),
                 reads=[b_aq], writes=[b_gg])
            S.op("dve", lambda v: v.scalar_tensor_tensor(out=ylgT[:, c, :], in0=gg[:, :], scalar=0.5, in1=hs[hq][:, :], op0=ALU.mult, op1=ALU.mult),
                 reads=[b_gg, b_hs[hq]], uwrites=[b_ylg])
            if DEBUG:
                S.dma("sp", lambda q_: q_.dma_start(out=dbg["hs"][:, c * W:(c + 1) * W], in_=hs[hq][:, :]), buf("dbg_hs"),
                      reads=[b_hs[hq]])

        def fe_next(k, c):
            if k >= 3:
                return
            tiles = {0: (0, 1), 1: (2, 3), 2: (4, 5), 3: (6,), 4: (7,), 5: (8,)}.get(c, ())
            for t in tiles:
                pending_tr.append(fe_norm(k + 1, t))

        def job_of(k, c):
            if k < 3:
                return J_pre[k][c // 2], (c % 2) * 128
            return J_lx[c], 0

        S.epoch(b_hT[0])
        for t in range(NT + 1):
            fe_tile(0, t)
        seq = [(k, c) for k in range(4) for c in range(8)]
        j0, cb0 = job_of(0, 0)
        lru_part1(0, 0, j0, cb0)
        lru_part1_ew(0, 0)
        j1, cb1 = job_of(0, 1)
        lru_part1(0, 1, j1, cb1)
        for i_, (k, c) in enumerate(seq):
            if c == 0 and k < 3:
                S.epoch(b_hT[(k + 1) % 2])
            if k == 3 and c == 0:
                if DEBUG:
                    S.dma("sp", lambda q: q.dma_start(out=dbg["hT"][:, :], in_=hT[1][:, :, :].rearrange("p k n -> p (k n)")),
                          buf("dbg_hT"), reads=[b_hT[1]])
                S.takeover(b_ylg, [b_hT[0]])
                S.takeover(b_z, [b_hT[0]])
            nxt = seq[i_ + 1] if i_ + 1 < len(seq) else None
            nxt2 = seq[i_ + 2] if i_ + 2 < len(seq) else None
            lru_part2a(k, c, (0,))
            if nxt is not None:
                lru_part1_ew_a(*nxt)
            lru_part2a(k, c, (1,))
            if nxt is not None:
                lru_part1_ew_b(*nxt)
            while pending_tr:
                fe_tr(*pending_tr.pop(0))
            if nxt2 is not None:
                j2, cb2 = job_of(*nxt2)
                lru_part1(nxt2[0], nxt2[1], j2, cb2)
            lru_part2b(k, c, job_of(k, c)[0])
            fe_next(k, c)

        b_sbb, b_sbc, b_sbcv, b_sbacc = buf("sb_b"), buf("sb_c"), buf("sb_cv"), buf("sb_acc")
        lru_bufs = [b_lx[0], b_lx[1], b_xc, b_rr, b_ii, b_aa, b_hs[0], b_hs[1], b_gg, b_xcb, b_xc2[1], b_rr2[1], b_xcb2[1]]
        cbufs_extra = [b_ii2[1], b_aa2[1]]
        S.takeover(b_sbb, lru_bufs)
        S.takeover(b_sbc, lru_bufs)
        S.takeover(b_sbcv, lru_bufs)
        S.takeover(b_sbacc, lru_bufs)
        S.op("dve", lambda v: v.memset(sb_acc[:, 0:2], 0.0), writes=[b_sbacc])
        def sc_chunk(c):
            res = {}
            for wi, nm in enumerate(("b", "c", "v")):
                ch = 3 * c + wi
                job = J_sc[ch // 2]
                slot = use_slab(job)
                wv = slab_view(slot, "K16")
                cb_ = (ch % 2) * 128
                sid = next_set()
                tk, b0, bks = mm_result(sid, lambda kk, wv=wv, cb_=cb_: wv[:, kk, cb_:cb_ + 128],
                                        lambda kk, g: hT[1][:, kk, g * GW:(g + 1) * GW], KC,
                                        reads=[slab_buf[slot], b_hT[1]])
                if ch % 2 == 1:
                    release_slab(job)
                res[nm] = (b0, bks)
                if nm == "b":
                    S.op("act", lambda a, b0=b0: a.activation(out=v3(sb_b[:, :]), in_=ps3(b0), func=AF.Copy),
                         reads=bks, writes=[b_sbb])
                elif nm == "c":
                    S.op("act", lambda a, b0=b0: a.activation(out=v3(sb_c[:, :]), in_=ps3(b0), func=AF.Copy),
                         reads=bks, writes=[b_sbc])
                else:
                    S.op("dve", lambda v, b0=b0: v.tensor_tensor(out=v3(sb_cv[:, :]), in0=ps3(b0), in1=v3(sb_c[:, :]), op=ALU.mult),
                         reads=bks + [b_sbc], writes=[b_sbcv])
            S.op("act", lambda a: a.activation(out=sb_acc[:, 2:W], in_=sb_cv[:, 2:W], func=AF.Copy, scale=colp(P_SCW + c * 3 + 2)),
                 reads=[b_sbcv, b_par], writes=[b_sbacc])
            for tap in (1, 0):
                sh = 2 - tap
                S.op("dve", lambda v, tap=tap, sh=sh: v.scalar_tensor_tensor(
                    out=sb_acc[:, 2:W], in0=sb_cv[:, 2 - sh:W - sh], scalar=colp(P_SCW + c * 3 + tap), in1=sb_acc[:, 2:W],
                    op0=ALU.mult, op1=ALU.add), reads=[b_sbcv, b_par], writes=[b_sbacc])
            S.op("dve", lambda v: v.tensor_tensor(out=zT[:, c, :], in0=sb_acc[:, :], in1=sb_b[:, :], op=ALU.mult),
                 reads=[b_sbacc, b_sbb], uwrites=[b_z])
        for c_ in range(8):
            sc_chunk(c_)
        if DEBUG:
            S.dma("sp", lambda q: q.dma_start(out=dbg["z"][:, :], in_=zT[:, :, :].rearrange("p k n -> p (k n)")), buf("dbg_z"), reads=[b_z])
            S.dma("sp", lambda q: q.dma_start(out=dbg["ylg"][:, :], in_=ylgT[:, :, :].rearrange("p k n -> p (k n)")), buf("dbg_ylg"), reads=[b_ylg])

        b_sgl, b_tm1, b_merged = buf("sgl"), buf("tm1"), buf("merged")
        sc_bufs = [b_sbb, b_sbc, b_sbcv, b_sbacc]
        S.takeover(b_sgl, lru_bufs + sc_bufs)
        S.takeover(b_tm1, lru_bufs + sc_bufs)
        S.takeover(b_merged, [b_xt[0], b_xt[1], b_ht[0], b_ht[1]] + cbufs_extra)
        def gate_j(j):
            jp = j // 2
            jig = J_ig[j]
            jop = J_op[jp]
            s_ig = use_slab(jig)
            wig = slab_view(s_ig, "K16")
            if j % 2 == 0:
                s_op = use_slab(jop)
            else:
                s_op = jop % NSLOT
            wop = slab_view(s_op, "K8")
            for br in range(2):
                sid = next_set()
                tk, gb0, gbks = mm_result(sid, lambda kk, br=br: wig[:, kk, br * 128:(br + 1) * 128],
                                          lambda kk, g: hT[1][:, kk, g * GW:(g + 1) * GW], KC,
                                          reads=[slab_buf[s_ig], b_hT[1]])
                S.op("act", lambda a, gb0=gb0: a.activation(out=v3(sgl[:, :]), in_=ps3(gb0), func=AF.Sigmoid),
                     reads=gbks, writes=[b_sgl])
                src = ylgT if br == 0 else zT
                bsrc = b_ylg if br == 0 else b_z
                cb_ = ((j % 2) * 2 + br) * 128
                sid = next_set()
                tk, yb0, ybks = mm_result(sid, lambda kk, cb_=cb_: wop[:, kk, cb_:cb_ + 128],
                                          lambda kk, g, src=src: src[:, kk, g * GW:(g + 1) * GW], 8,
                                          reads=[slab_buf[s_op], bsrc])
                if br == 0:
                    S.op("dve", lambda v, yb0=yb0: v.tensor_tensor(out=v3(tm1[:, :]), in0=ps3(yb0), in1=v3(sgl[:, :]), op=ALU.mult),
                         reads=ybks + [b_sgl], writes=[b_tm1])
                else:
                    S.op("dve", lambda v, yb0=yb0: v.tensor_tensor(out=v3(sgl[:, :]), in0=ps3(yb0), in1=v3(sgl[:, :]), op=ALU.mult),
                         reads=ybks, writes=[b_sgl])
                    S.op("dve", lambda v: v.tensor_tensor(out=merged[:, j, :], in0=sgl[:, :], in1=tm1[:, :], op=ALU.add),
                         reads=[b_sgl, b_tm1], uwrites=[b_merged])
            release_slab(jig)
            if j % 2 == 1:
                release_slab(jop)
        for j_ in range(16):
            gate_j(j_)
        if DEBUG:
            S.dma("sp", lambda q: q.dma_start(out=dbg["merged"][:, :], in_=merged[:, :, :].rearrange("p k n -> p (k n)")), buf("dbg_m"), reads=[b_merged])

        b_xr = [buf("xr%d" % t) for t in range(NT)]
        b_xrh = buf("xrh")
        olds = [b_hT[0], b_hT[1], b_ylg, b_z]
        for t in range(NT):
            S.takeover(b_xr[t], olds)
        S.takeover(b_xrh, lru_bufs + sc_bufs + [b_sgl, b_tm1])
        for t in range(NT):
            r0 = 3 * 1024 + HALO + 128 * t
            S.dma("sp", lambda q, t=t, r0=r0: q.dma_start(out=xr[:, t, :], in_=xw[r0:r0 + 128, :]), b_xr[t], writes=[b_xr[t]])
        S.dma("sp", lambda q: q.dma_start(out=xrh[0:HALO, :], in_=xw[3 * 1024:3 * 1024 + HALO, :]), b_xrh, writes=[b_xrh])
        S.dma("sp", lambda q: q.dma_start(out=gB[:, :], in_=g3_d[1, :, :]), b_gB, writes=[b_gB])

        def wo_block(cbk):
            ja, jb = J_wo[2 * cbk], J_wo[2 * cbk + 1]
            sa = use_slab(ja)
            sb = use_slab(jb)
            wa_v, wb_v = slab_view(sa, "K16"), slab_view(sb, "K16")
            for t in range(NT + 1):
                ntok = HALO if t == 0 else 128
                col0 = 0 if t == 0 else HALO + 128 * (t - 1)
                bk = 2 * (rot["wo"] % 3)
                rot["wo"] += 1

                def fn(pe, ntok=ntok, col0=col0, bk=bk):
                    last = None
                    for kk in range(KC):
                        l_ = merged[:, kk, col0:col0 + ntok]
                        pe.matmul(ps[0:ntok, bk, 0:256], lhsT=l_, rhs=wa_v[:, kk, :], start=(kk == 0), stop=(kk == KC - 1))
                        last = pe.matmul(ps[0:ntok, bk + 1, 0:256], lhsT=l_, rhs=wb_v[:, kk, :], start=(kk == 0), stop=(kk == KC - 1))
                    return last
                S.op("pe", fn, reads=[b_merged, slab_buf[sa], slab_buf[sb]], writes=[bank[bk], bank[bk + 1]])
                if t == 0:
                    dst = xrh[0:HALO, cbk * 512:(cbk + 1) * 512]
                    bd = b_xrh
                else:
                    dst = xr[:, t - 1, cbk * 512:(cbk + 1) * 512]
                    bd = b_xr[t - 1]
                dst2 = dst.rearrange("p (b n) -> p b n", b=2)
                S.op("dve", lambda v, dst2=dst2, ntok=ntok, bk=bk: v.tensor_tensor(out=dst2, in0=ps[0:ntok, bk:bk + 2, 0:256], in1=dst2, op=ALU.add),
                     reads=[bank[bk], bank[bk + 1]], writes=[bd])
            release_slab(ja)
            release_slab(jb)
        for cbk_ in range(4):
            wo_block(cbk_)
        if DEBUG:
            S.dma("sp", lambda q: q.dma_start(out=dbg["xr"][:, :], in_=xr[:, :, :].rearrange("p k n -> p (k n)")), buf("dbg_xr"), reads=b_xr)

        b_h2t, b_sqj, b_h2T = [buf("h2t0"), buf("h2t1")], buf("sqj"), buf("h2T")
        S.takeover(b_h2T, [b_merged])
        for q_ in range(2):
            S.takeover(b_h2t[q_], lru_bufs + sc_bufs + [b_sgl, b_tm1])
        S.takeover(b_sqj, lru_bufs + sc_bufs + [b_sgl, b_tm1])
        for t in range(NT + 1):
            ntok = HALO if t == 0 else 128
            col0 = 0 if t == 0 else HALO + 128 * (t - 1)
            src = xrh if t == 0 else xr[:, t - 1, :]
            bsrc = b_xrh if t == 0 else b_xr[t - 1]
            q_ = t % 2
            rmsnorm_tile(src, ntok, h2t[q_], sqj, bsrc, b_h2t[q_], b_sqj)
            transpose_tile(h2t[q_], ntok, h2T, col0, b_h2t[q_], b_h2T)
        if DEBUG:
            S.dma("sp", lambda q: q.dma_start(out=dbg["h2T"][:, :], in_=h2T[:, :, :].rearrange("p k n -> p (k n)")), buf("dbg_h2T"), reads=[b_h2T])
        S.dma("sp", lambda q: q.dma_start(out=gB[:, :], in_=g3_d[2, :, :]), b_gB, writes=[b_gB])

        b_gated = [buf("gated0"), buf("gated1")]
        b_upg, b_upv, b_acg, b_acv, b_sgf = buf("upg"), buf("upv"), buf("acg"), buf("acv"), buf("sgf")
        old_d = lru_bufs + sc_bufs + [b_sgl, b_tm1, b_xrh, b_h2t[0], b_h2t[1], b_sqj]
        for bb_ in (b_gated[0], b_gated[1], b_upg, b_upv, b_acg, b_acv, b_sgf):
            S.takeover(bb_, old_d)

        def ffn_up(G):
            gs = G % 2
            for cl in range(FG):
                cch = G * FG + cl
                job = J_up[G][cl]
                slot = use_slab(job)
                wv = slab_view(slot, "K16")
                bb = {}
                for wi in range(2):
                    sid = next_set()
                    tk, b0, bks = mm_result(sid, lambda kk, wi=wi, wv=wv: wv[:, kk, wi * 128:(wi + 1) * 128],
                                            lambda kk, g: h2T[:, kk, g * GW:(g + 1) * GW], KC,
                                            reads=[slab_buf[slot], b_h2T])
                    raw = upg if wi == 0 else upv
                    acc = acg if wi == 0 else acv
                    braw = b_upg if wi == 0 else b_upv
                    bacc = b_acg if wi == 0 else b_acv
                    pc = P_FCW + ((cch if wi == 0 else NFC + cch)) * 3
                    S.op("act", lambda a, raw=raw, b0=b0: a.activation(out=v3(raw[:, :]), in_=ps3(b0), func=AF.Copy),
                         reads=bks, writes=[braw])
                    S.op("act", lambda a, acc=acc, b0=b0, pc=pc: a.activation(out=v3(acc[:, :]), in_=ps3(b0), func=AF.Copy,
                                                                              scale=colp(pc + 2)),
                         reads=bks + [b_par], writes=[bacc])
                    bb[wi] = (raw, acc, braw, bacc, pc)
                release_slab(job)
                for tap in (1, 0):
                    for wi in range(2):
                        raw, acc, braw, bacc, pc = bb[wi]
                        sh = 2 - tap
                        S.op("dve", lambda v, raw=raw, acc=acc, pc=pc, tap=tap, sh=sh: v.scalar_tensor_tensor(
                            out=acc[:, HALO:W], in0=raw[:, HALO - sh:W - sh], scalar=colp(pc + tap), in1=acc[:, HALO:W],
                            op0=ALU.mult, op1=ALU.add), reads=[braw, b_par], writes=[bacc])
                S.op("act", lambda a: a.activation(out=sgf[:, HALO:W], in_=acg[:, HALO:W], func=AF.Silu),
                     reads=[b_acg], writes=[b_sgf])
                S.op("dve", lambda v, gs=gs, cl=cl: v.tensor_tensor(out=gated[gs][:, cl, :], in0=sgf[:, HALO:W], in1=acv[:, HALO:W], op=ALU.mult),
                     reads=[b_sgf, b_acv], uwrites=[b_gated[gs]])

        dn_rot = {"n": 0}

        def ffn_down(G, final):
            gs = G % 2
            jd = J_dn[G]
            s0 = use_slab(jd[0])
            s1 = use_slab(jd[1])
            wd = [slab_view(s0, "DN"), slab_view(s1, "DN")]
            for tt in range(NT):
                for hlf in range(2):
                    pair = (3 + dn_rot["n"]) % 4
                    dn_rot["n"] += 1
                    d0 = 2 * pair
                    bks = bank[d0:d0 + 2]

                    def fn(pe, tt=tt, d0=d0, hlf=hlf):
                        last = None
                        for cl in range(FG):
                            l_ = gated[gs][:, cl, tt * 128:(tt + 1) * 128]
                            for nb in range(2):
                                c0 = (2 * hlf + nb) * 512
                                last = pe.matmul(ps[:, d0 + nb, :], lhsT=l_, rhs=wd[cl // 2][:, cl % 2, c0:c0 + 512],
                                                 start=(cl == 0), stop=(cl == FG - 1))
                        return last
                    S.op("pe", fn, reads=[b_gated[gs], slab_buf[s0], slab_buf[s1]], writes=bks)
                    S.op("dve", lambda v, tt=tt, d0=d0, hlf=hlf: v.tensor_tensor(
                        out=xr[:, tt, hlf * 1024:(hlf + 1) * 1024].rearrange("p (b n) -> p b n", b=2),
                        in0=ps[:, d0:d0 + 2, :],
                        in1=xr[:, tt, hlf * 1024:(hlf + 1) * 1024].rearrange("p (b n) -> p b n", b=2), op=ALU.add),
                        reads=bks, writes=[b_xr[tt]])
                if final:
                    final_tile(tt)
            release_slab(jd[0])
            release_slab(jd[1])

        b_ostg = [buf("ostg0"), buf("ostg1")]
        fin = {"init": False}

        def final_tile(tt):
            if not fin["init"]:
                fin["init"] = True
                S.takeover(b_ostg[0], [b_gated[0], b_gated[1]])
                S.takeover(b_ostg[1], [b_upg, b_upv, b_acg, b_acv, b_sgf])
                S.takeover(b_sqj, [b_gated[0], b_gated[1], b_upg, b_upv, b_acg, b_acv, b_sgf, b_sqj])
            q_ = tt % 2
            qq = fe_cnt["n"] % 2
            fe_cnt["n"] += 1
            ssc = sm[:, 40 + qq:41 + qq]
            ss2 = sm[:, 42 + qq:43 + qq]
            rs_ = sm[:, 44 + qq:45 + qq]
            bq = buf("ssq%d" % qq)
            S.op("act", lambda a: a.activation(out=sqj[:, :], in_=xr[:, tt, :], func=AF.Square, accum_out=ssc),
                 reads=[b_xr[tt]], writes=[b_sqj, bq])
            S.op("act", lambda a: a.activation(out=ss2, in_=ssc, func=AF.Copy), writes=[bq])
            S.op("act", lambda a: a.activation(out=rs_, in_=ss2, func=AF.Sqrt, scale=1.0 / D, bias=eps_ap), writes=[bq])
            S.op("dve", lambda v: v.reciprocal(out=rs_, in_=rs_), writes=[bq])
            S.op("dve", lambda v: v.scalar_tensor_tensor(out=ostg[q_][:, :], in0=xr[:, tt, :], scalar=rs_, in1=gB[:, :],
                                                        op0=ALU.mult, op1=ALU.mult),
                 reads=[b_xr[tt], bq, b_gB], writes=[b_ostg[q_]])
            tok = S.dma("sp", lambda q: q.dma_start(out=out_d[tt * 128:(tt + 1) * 128, :], in_=ostg[q_][:, :]), buf("ost%d" % q_),
                        reads=[b_ostg[q_]])
            out_toks.append(tok)

        out_toks = []
        for G in range(NGRP):
            ffn_up(G)
            if G >= 1:
                ffn_down(G - 1, False)
        ffn_down(NGRP - 1, True)
        S.wait_only("sp", out_toks)
        assert state_w["next_issue"] == len(jobs), (state_w["next_issue"], len(jobs))

        block = es.enter_context(nc.Block())
        S.replay(block)
    return nc


_CACHE = {}


def _perm_cols(w, order, width=128):
    return np.ascontiguousarray(np.concatenate([w[:, c * width:(c + 1) * width] for c in order], axis=1))


def kernel(x, g_mix, w_in, lru_conv_w, lru_conv_b, lru_wa, lru_ba, lru_wx, lru_bx, lru_lambda,
           lru_w_out, sc_conv_w, sc_w_out, w_o, g_ffn, ffn_w_up, ffn_conv_w, ffn_w_down, g_final):
    f = lambda a: np.ascontiguousarray(np.asarray(a, dtype=np.float32))
    x = f(x); w_in = f(w_in)
    Bsz, Sq, Dm = x.shape
    assert (Bsz, Sq, Dm) == (2, 4096, 2048)
    if "nc" not in _CACHE:
        _CACHE["nc"] = build_program()
    nc = _CACHE["nc"]

    order = []
    for c in range(8):
        order += [c, 8 + c]
    for c in range(8):
        order += [16 + c, 24 + c, 32 + c]
    for j in range(16):
        order += [40 + j, 56 + j]
    def slabs_k(wm, ncols):
        K_, N_ = wm.shape
        return np.ascontiguousarray(wm.reshape(K_ // 128, 128, N_ // ncols, ncols).transpose(2, 1, 0, 3).reshape(N_ // ncols, 128, -1))
    w_inp = slabs_k(_perm_cols(w_in, order), 256)
    w_lx = slabs_k(w_in[:, 0:1024], 256)
    lw, sw = f(lru_w_out), f(sc_w_out)
    w_oc = slabs_k(np.concatenate(
        [blk for j in range(16) for blk in (lw[:, j * 128:(j + 1) * 128], sw[:, j * 128:(j + 1) * 128])], axis=1), 512)
    wu = f(ffn_w_up)
    w_upp = slabs_k(np.concatenate(
        [blk for c in range(NFC) for blk in (wu[:, c * 128:(c + 1) * 128], wu[:, DFF + c * 128:DFF + (c + 1) * 128])], axis=1), 256)
    w_dn = np.ascontiguousarray(f(ffn_w_down).reshape(22, 2, 128, D).transpose(0, 2, 1, 3).reshape(22, 128, 4096))
    w_o_ = slabs_k(f(w_o), 256)

    def blkdiag(wg):
        wg = f(wg)
        o = np.zeros((128, 8, 128), np.float32)
        for c in range(8):
            o[0:64, c, 0:64] = wg[2 * c]
            o[64:128, c, 64:128] = wg[2 * c + 1]
        return o.reshape(128, 1024)
    gwa, gwx = blkdiag(lru_wa), blkdiag(lru_wx)

    def fm(v, n):
        return f(v).reshape(n, 128).T

    g3 = np.ascontiguousarray(np.stack([np.broadcast_to(f(g)[None, :], (128, D)) for g in (g_mix, g_ffn, g_final)]))
    ident = np.eye(128, dtype=np.float32)

    in_maps = []
    for core in range(8):
        b, j = core // 4, core % 4
        par = np.zeros((128, NPAR), np.float32)
        cw = f(lru_conv_w)
        par[:, P_CW4:P_CW4 + 32] = cw.reshape(4, 8, 128).transpose(2, 1, 0).reshape(128, 32)
        par[:, P_CB:P_CB + 8] = fm(lru_conv_b, 8)
        par[:, P_BA:P_BA + 8] = fm(lru_ba, 8)
        par[:, P_BX:P_BX + 8] = fm(lru_bx, 8)
        par[:, P_LAM:P_LAM + 8] = fm(lru_lambda, 8)
        sw3 = f(sc_conv_w)
        par[:, P_SCW:P_SCW + 24] = sw3.reshape(3, 8, 128).transpose(2, 1, 0).reshape(128, 24)
        for k in range(3):
            par[:, P_MASK + k] = 1.0 if (j + k - 3) >= 0 else 0.0
        par[:, P_MASK + 3] = 1.0
        fw = f(ffn_conv_w)
        par[:, P_FCW:P_FCW + 264] = fw.reshape(3, 88, 128).transpose(2, 1, 0).reshape(128, 264)
        xwin = np.zeros((4 * 1024 + HALO, D), np.float32)
        t0 = 1024 * j
        g_lo = t0 - 3072 - HALO
        lo = max(g_lo, 0)
        xwin[lo - g_lo:, :] = x[b, lo:t0 + 1024, :]
        in_maps.append({"xw": xwin, "w_lx": w_lx, "w_inp": w_inp, "w_oc": w_oc, "w_o": w_o_, "w_up": w_upp,
                        "w_dn": w_dn, "gwa": gwa, "gwx": gwx, "par": par, "g3": g3, "ident": ident})
    res = run_bass_kernel_spmd(nc, in_maps, core_ids=list(range(8)))
    _CACHE["res"] = res
    out = np.zeros((Bsz, Sq, Dm), np.float32)
    for core in range(8):
        b, j = core // 4, core % 4
        out[b, 1024 * j:1024 * (j + 1), :] = np.asarray(res.results[core]["out"], dtype=np.float32)
    return out
```

```python
import numpy as np
from contextlib import ExitStack
import concourse.bass as bass
import concourse.mybir as mybir
from concourse.bass_utils import run_bass_kernel_spmd

F32 = mybir.dt.float32
BF16 = mybir.dt.bfloat16
U8 = mybir.dt.uint8
AF = mybir.ActivationFunctionType
ALU = mybir.AluOpType

D = 2048
KC = 16
W = 1032
NG = 3
GW = 344
NT = 8
HALO = 8
DFF = 5632
NFC = 44
FG = 4
NGRP = 11
EPS = 1e-6
NSLOT = 5
SLAB = 8192

P_CW4, P_CB, P_BA, P_BX, P_LAM, P_SCW, P_MASK, P_FCW = 0, 32, 40, 48, 56, 64, 88, 92
NPAR = 92 + 88 * 3

DEBUG = False


class Buf:
    def __init__(self, name):
        self.name = name
        self.prev = []
        self.ws = []
        self.rs = []
        self.dsem = None
        self.dcnt = 0

    def all_tokens(self):
        return list(self.prev) + list(self.ws) + list(self.rs)


class Sched:
    def __init__(self, nc, es):
        self.nc = nc
        self.es = es
        self.names = ["pe", "act", "dve", "pool", "sp"]
        self.sem = {k: es.enter_context(nc.semaphore("s_" + k)) for k in self.names}
        self.cnt = {k: 0 for k in self.names}
        self.seen = {k: {} for k in self.names}
        self.prog = {k: [] for k in self.names}
        self.nsem = 0

    def _deps(self, reads, writes, uwrites, extra):
        toks = []
        for b in reads:
            toks += b.prev + b.ws
        for b in writes:
            toks += b.prev + b.ws + b.rs
        for b in uwrites:
            toks += b.prev
        toks += list(extra)
        return toks

    def _emit_waits(self, e, toks):
        need = {}
        for t in toks:
            if t is None:
                continue
            sem, val = t
            key = sem.name
            if self.seen[e].get(key, 0) >= val:
                continue
            if key not in need or need[key][1] < val:
                need[key] = (sem, val)
        for key, (sem, val) in need.items():
            self.prog[e].append(("wait", sem, val))
            self.seen[e][key] = val

    def _post(self, tok, reads, writes, uwrites):
        for b in reads:
            b.rs.append(tok)
        for b in writes:
            b.prev = []
            b.ws = [tok]
            b.rs = []
        for b in uwrites:
            b.ws.append(tok)

    def op(self, e, fn, reads=(), writes=(), uwrites=(), extra=()):
        toks = self._deps(reads, writes, uwrites, extra)
        self._emit_waits(e, toks)
        self.cnt[e] += 1
        tok = (self.sem[e], self.cnt[e])
        self.prog[e].append(("op", fn, self.sem[e], 1))
        self._post(tok, reads, writes, uwrites)
        return tok

    def dma(self, e, fn, dbuf, reads=(), writes=(), uwrites=(), extra=()):
        if dbuf.dsem is None:
            dbuf.dsem = self.es.enter_context(self.nc.semaphore("d_" + dbuf.name))
            self.nsem += 1
        toks = self._deps(reads, writes, uwrites, extra)
        self._emit_waits(e, toks)
        dbuf.dcnt += 16
        tok = (dbuf.dsem, dbuf.dcnt)
        self.prog[e].append(("op", fn, dbuf.dsem, 16))
        self._post(tok, reads, writes, uwrites)
        return tok

    def wait_only(self, e, toks):
        self._emit_waits(e, toks)

    def epoch(self, b):
        b.prev = b.all_tokens()
        b.ws = []
        b.rs = []

    def takeover(self, new, olds):
        toks = list(new.all_tokens())
        for o in olds:
            toks += o.all_tokens()
        new.prev = toks
        new.ws = []
        new.rs = []

    def replay(self, block):
        nc = self.nc

        def mk(name):
            items = self.prog[name]

            def body(eng):
                for it in items:
                    if it[0] == "wait":
                        eng.wait_ge(it[1], it[2])
                    else:
                        ins = it[1](eng)
                        ins.then_inc(it[2], it[3])
            return body

        block.tensor(mk("pe"))
        block.scalar(mk("act"))
        block.vector(mk("dve"))
        block.gpsimd(mk("pool"))
        block.sync(mk("sp"))


def build_program():
    nc = bass.Bass("TRN2", target_bir_lowering=False)
    xw = nc.dram_tensor("xw", [4 * 1024 + HALO, D], F32, kind="ExternalInput").ap()
    w_lx = nc.dram_tensor("w_lx", [4, 128, 4096], F32, kind="ExternalInput").ap()
    w_inp = nc.dram_tensor("w_inp", [36, 128, 4096], F32, kind="ExternalInput").ap()
    w_oc = nc.dram_tensor("w_oc", [8, 128, 4096], F32, kind="ExternalInput").ap()
    w_o = nc.dram_tensor("w_o", [8, 128, 4096], F32, kind="ExternalInput").ap()
    w_up = nc.dram_tensor("w_up", [44, 128, 4096], F32, kind="ExternalInput").ap()
    w_dn = nc.dram_tensor("w_dn", [22, 128, 4096], F32, kind="ExternalInput").ap()
    gwa_d = nc.dram_tensor("gwa", [128, 8 * 128], F32, kind="ExternalInput").ap()
    gwx_d = nc.dram_tensor("gwx", [128, 8 * 128], F32, kind="ExternalInput").ap()
    par_d = nc.dram_tensor("par", [128, NPAR], F32, kind="ExternalInput").ap()
    g3_d = nc.dram_tensor("g3", [3, 128, D], F32, kind="ExternalInput").ap()
    id_d = nc.dram_tensor("ident", [128, 128], F32, kind="ExternalInput").ap()
    out_d = nc.dram_tensor("out", [1024, D], F32, kind="ExternalOutput").ap()
    dbg = {}
    if DEBUG:
        dbg["hT"] = nc.dram_tensor("dbg_hT", [128, KC * W], BF16, kind="ExternalOutput").ap()
        dbg["hs"] = nc.dram_tensor("dbg_hs", [128, 8 * W], F32, kind="ExternalOutput").ap()
        dbg["z"] = nc.dram_tensor("dbg_z", [128, 8 * W], BF16, kind="ExternalOutput").ap()
        dbg["ylg"] = nc.dram_tensor("dbg_ylg", [128, 8 * W], BF16, kind="ExternalOutput").ap()
        dbg["merged"] = nc.dram_tensor("dbg_merged", [128, KC * W], BF16, kind="ExternalOutput").ap()
        dbg["xr"] = nc.dram_tensor("dbg_xr", [128, NT * D], F32, kind="ExternalOutput").ap()
        dbg["h2T"] = nc.dram_tensor("dbg_h2T", [128, KC * W], BF16, kind="ExternalOutput").ap()

    with ExitStack() as es:
        base0 = (nc.sbuf_base + 31) // 32 * 32
        ARENA = 207872
        arena = nc.alloc_sbuf_tensor("arena", [128, ARENA], U8)
        cur = [base0]

        def at(name, shape, dt, off):
            return nc.alloc_sbuf_tensor_at(name, shape, dt, offset=off)

        R_A = base0
        R_B = R_A + 33024
        R_C = R_B + 33024
        R_D = R_C + 33024
        D_SIZE = 50176
        R_E = R_D + D_SIZE
        R_P = R_E + NSLOT * SLAB
        assert R_P + 15872 <= base0 + ARENA, (R_P + 15872, base0 + ARENA)

        hT = [at("hT0", [128, KC, W], BF16, R_A), at("hT1", [128, KC, W], BF16, R_B)]
        zT = at("zT", [128, 8, W], BF16, R_A)
        ylgT = at("ylgT", [128, 8, W], BF16, R_A + 16512)
        xr = at("xr", [128, NT, D], F32, R_A)
        xt = [at("xt0", [128, D], F32, R_C), at("xt1", [128, D], F32, R_C + 8192)]
        ht = [at("ht0", [128, D], BF16, R_C + 16384), at("ht1", [128, D], BF16, R_C + 20480)]
        merged = at("merged", [128, KC, W], BF16, R_C)
        h2T = at("h2T", [128, KC, W], BF16, R_C)
        o = R_D
        lx = [at("lx0", [128, W], F32, o), at("lx1", [128, W], F32, o + 4128)]
        xc = at("xc", [128, W], F32, o + 2 * 4128)
        rr = at("rr", [128, W], F32, o + 3 * 4128)
        xc2 = [xc, at("xc_b", [128, W], F32, o + 9 * 4128 + 2080)]
        rr2 = [rr, at("rr_b", [128, W], F32, o + 10 * 4128 + 2080)]
        xcb2 = [at("xcb", [128, W], BF16, o + 9 * 4128), at("xcb_b", [128, W], BF16, o + 11 * 4128 + 2080)]
        assert 11 * 4128 + 2080 + 2064 <= D_SIZE
        ii = at("ii", [128, W], F32, o + 4 * 4128)
        aa = at("aa", [128, W], F32, o + 5 * 4128)
        hs = [at("hs0", [128, W], F32, o + 6 * 4128), at("hs1", [128, W], F32, o + 7 * 4128)]
        gg = at("gg", [128, W], F32, o + 8 * 4128)
        ii2 = [ii, at("ii_b", [128, W], F32, R_C + 24576)]
        aa2 = [aa, at("aa_b", [128, W], F32, R_C + 24576 + 4128)]
        assert 24576 + 2 * 4128 <= 33024
        assert 9 * 4128 + 2064 <= D_SIZE
        sb_b = at("sb_b", [128, W], F32, o)
        sb_c = at("sb_c", [128, W], F32, o + 4128)
        sb_cv = at("sb_cv", [128, W], F32, o + 2 * 4128)
        sb_acc = at("sb_acc", [128, W], F32, o + 3 * 4128)
        sgl = at("sgl", [128, W], F32, o + 4 * 4128)
        tm1 = at("tm1", [128, W], F32, o + 5 * 4128)
        xrh = at("xrh", [128, D], F32, o)
        h2t = [at("h2t0", [128, D], BF16, o + 8192), at("h2t1", [128, D], BF16, o + 12288)]
        sqj = at("sqj", [128, D], BF16, o + 16384)
        gated = [at("gated0", [128, FG, 1024], BF16, o + 20736), at("gated1", [128, FG, 1024], BF16, o + 28928)]
        upg = at("upg", [128, W], F32, o)
        upv = at("upv", [128, W], F32, o + 4128)
        acg = at("acg", [128, W], F32, o + 2 * 4128)
        acv = at("acv", [128, W], F32, o + 3 * 4128)
        sgf = at("sgf", [128, W], F32, o + 4 * 4128)
        assert 5 * 4128 <= 20736
        ostg = [at("ostg0", [128, D], F32, o + 37120), at("ostg1", [128, D], F32, o)]
        assert 37120 + 8192 <= D_SIZE
        slab = [at("slab%d" % i, [128, SLAB // 2], BF16, R_E + i * SLAB) for i in range(NSLOT)]
        o = R_P
        gB = at("gB", [128, D], F32, o); o += 8192
        par = at("par_s", [128, NPAR], F32, o); o += ((NPAR * 4 + 31) // 32) * 32
        gwa = at("gwa_s", [128, 8, 128], BF16, o); o += 2048
        gwx = at("gwx_s", [128, 8, 128], BF16, o); o += 2048
        ident = at("ident_s", [128, 128], BF16, o); o += 256
        sm = at("small", [128, 128], F32, o); o += 512
        assert o <= R_P + 15872
        coef = sm[:, 0:8]
        eps_ap = sm[:, 72:73]
        coef2 = sm[:, 8:16]
        state = sm[:, 16:24]
        hh = sm[:, 24:40]

        ps = nc.alloc_psum_tensor("ps", [128, 8, 512], F32)

        S = Sched(nc, es)
        B = {}
        def buf(name):
            if name not in B:
                B[name] = Buf(name)
            return B[name]
        bank = [buf("bank%d" % i) for i in range(8)]

        jobs = []
        slab_buf = [buf("slab%d" % i) for i in range(NSLOT)]
        state_w = {"next_issue": 0}

        def add_job(kind, src, a, b_):
            jobs.append((kind, src, a, b_))
            return len(jobs) - 1

        def slab_view(slot, kind):
            s = slab[slot]
            if kind == "K16":
                return s[:, :].rearrange("p (k n) -> p k n", k=16)
            if kind == "K8":
                return s[:, :].rearrange("p (k n) -> p k n", k=8)
            if kind == "DN":
                return s[:, :].rearrange("p (k n) -> p k n", k=2)
            raise ValueError(kind)

        def issue_job(j):
            kind, src, a, b_ = jobs[j]
            slot = j % NSLOT
            view = slab[slot][:, :].rearrange("p (a n) -> p a n", a=2)
            src_ap = src[a, :, :].rearrange("p (a n) -> p a n", a=2)
            S.dma("pool", lambda g, v=view, s_=src_ap: g.dma_start(out=v, in_=s_), slab_buf[slot],
                  writes=[slab_buf[slot]])

        def ensure_issued(upto):
            while state_w["next_issue"] <= min(upto, len(jobs) - 1):
                issue_job(state_w["next_issue"])
                state_w["next_issue"] += 1

        def use_slab(j):
            ensure_issued(j)
            return j % NSLOT

        def release_slab(j):
            ensure_issued(j + NSLOT)

        J_pre = [[add_job("K16", w_lx, s, 0) for s in range(4)] for k in range(3)]
        J_lx = [add_job("K16", w_inp, c, 0) for c in range(8)]
        J_sc = [add_job("K16", w_inp, 8 + s, 0) for s in range(12)]
        J_ig, J_op = [], []
        for jp in range(8):
            J_ig.append(add_job("K16", w_inp, 20 + 2 * jp, 0))
            J_op.append(add_job("K8", w_oc, jp, 0))
            J_ig.append(add_job("K16", w_inp, 20 + 2 * jp + 1, 0))
        J_wo = [add_job("K16", w_o, s, 0) for s in range(8)]
        J_up, J_dn = [], []
        for G in range(NGRP):
            J_up.append([add_job("K16", w_up, G * FG + c, 0) for c in range(FG)])
            if G >= 1:
                J_dn.append([add_job("DN", w_dn, (G - 1) * 2 + h, 0) for h in range(2)])
        J_dn.append([add_job("DN", w_dn, (NGRP - 1) * 2 + h, 0) for h in range(2)])

        b_par, b_gB, b_gw, b_id, b_sm = buf("par"), buf("gB"), buf("gw"), buf("ident"), buf("sm")
        S.dma("sp", lambda q: q.dma_start(out=par[:, :], in_=par_d[:, :]), b_par, writes=[b_par])
        S.dma("sp", lambda q: q.dma_start(out=gB[:, :], in_=g3_d[0, :, :]), b_gB, writes=[b_gB])
        S.dma("pool", lambda q: q.dma_start(out=ident[:, :], in_=id_d[:, :]), b_id, writes=[b_id])
        S.dma("pool", lambda q: q.dma_start(out=gwa[:, :, :].rearrange("p k n -> p (k n)"), in_=gwa_d[:, :]), b_gw, writes=[b_gw])
        S.dma("pool", lambda q: q.dma_start(out=gwx[:, :, :].rearrange("p k n -> p (k n)"), in_=gwx_d[:, :]), b_gw, uwrites=[b_gw])
        ensure_issued(NSLOT - 1)
        S.op("act", lambda a: a.activation(out=sm[:, 48:56], in_=par[:, P_LAM:P_LAM + 8], func=AF.Exp, scale=-1.0),
             reads=[b_par], writes=[b_sm])
        S.op("act", lambda a: a.activation(out=sm[:, 48:56], in_=sm[:, 48:56], func=AF.Ln, bias=1.0, scale=1.0),
             writes=[b_sm])
        S.op("dve", lambda v: v.tensor_scalar(out=coef, in0=sm[:, 48:56], scalar1=-8.0, scalar2=None, op0=ALU.mult),
             writes=[b_sm])
        S.op("dve", lambda v: v.tensor_scalar(out=coef2, in0=sm[:, 48:56], scalar1=-4.0, scalar2=None, op0=ALU.mult),
             writes=[b_sm])
        S.op("dve", lambda v: v.tensor_scalar(out=sm[:, 56:64], in0=par[:, P_BA:P_BA + 8], scalar1=0.5, scalar2=None, op0=ALU.mult),
             reads=[b_par], writes=[b_sm])
        S.op("dve", lambda v: v.tensor_scalar(out=sm[:, 64:72], in0=par[:, P_BX:P_BX + 8], scalar1=0.5, scalar2=None, op0=ALU.mult),
             reads=[b_par], writes=[b_sm])
        S.op("dve", lambda v: v.memset(sm[:, 16:40], 0.0), writes=[b_sm])
        b_eps = buf("eps")
        S.op("dve", lambda v: v.memset(sm[:, 72:73], EPS), writes=[b_eps])
        b_xc, b_hs = buf("xc"), [buf("hs0"), buf("hs1")]
        S.op("dve", lambda v: v.memset(xc[:, 0:8], 0.0), writes=[b_xc])
        for s_ in range(2):
            S.op("dve", lambda v, s_=s_: v.memset(hs[s_][:, 0:8], 0.0), writes=[b_hs[s_]])

        rot = {"set": 0, "wo": 0}

        def next_set():
            s_ = rot["set"]
            rot["set"] ^= 1
            return s_

        def mm_result(setid, lhs_fn, rhs_fn, nk, reads, extra=()):
            b0 = 3 * setid
            banks = bank[b0:b0 + 3]

            def fn(t):
                last = None
                for k in range(nk):
                    l_ = lhs_fn(k)
                    for g in range(NG):
                        last = t.matmul(ps[:, b0 + g, 0:GW], lhsT=l_, rhs=rhs_fn(k, g),
                                        start=(k == 0), stop=(k == nk - 1))
                return last
            return S.op("pe", fn, reads=reads, writes=banks, extra=extra), b0, banks

        def ps3(b0):
            return ps[:, b0:b0 + 3, 0:GW]

        def v3(t_ap):
            return t_ap.rearrange("p (g n) -> p g n", g=NG)

        def colp(c0):
            return par[:, c0:c0 + 1]

        b_xt, b_ht = [buf("xt0"), buf("xt1")], [buf("ht0"), buf("ht1")]
        b_hT = [buf("hT0"), buf("hT1")]
        fe_cnt = {"n": 0}

        def rmsnorm_tile(src_tile, ntok, dst_bf, junk_bf, b_src, b_dst, b_junk):
            q = fe_cnt["n"] % 2
            fe_cnt["n"] += 1
            ssc = sm[0:ntok, 40 + q:41 + q]
            ss2 = sm[0:ntok, 42 + q:43 + q]
            rs_ = sm[0:ntok, 44 + q:45 + q]
            bq = buf("ssq%d" % q)
            S.op("act", lambda a: a.activation(out=junk_bf[0:ntok, :], in_=src_tile[0:ntok, :], func=AF.Square, accum_out=ssc),
                 reads=[b_src], writes=[b_junk, bq])
            S.op("act", lambda a: a.activation(out=ss2, in_=ssc, func=AF.Copy), writes=[bq])
            S.op("act", lambda a: a.activation(out=rs_, in_=ss2, func=AF.Sqrt, scale=1.0 / D, bias=sm[0:ntok, 72:73]),
                 reads=[b_eps], writes=[bq])
            S.op("dve", lambda v: v.reciprocal(out=rs_, in_=rs_), writes=[bq])
            S.op("dve", lambda v: v.scalar_tensor_tensor(out=dst_bf[0:ntok, :], in0=src_tile[0:ntok, :], scalar=rs_,
                                                        in1=gB[0:ntok, :], op0=ALU.mult, op1=ALU.mult),
                 reads=[b_src, bq, b_gB], writes=[b_dst])
            return rs_, bq

        def transpose_tile(src_bf, ntok, dstT, col0, b_src, b_dstT, act_half0=True):
            for half in range(2):
                bk = 6 + half
                pb = ps[:, bk, :].bitcast(BF16)

                def fn(t, half=half, pb=pb):
                    last = None
                    for i in range(8):
                        cc = half * 8 + i
                        last = t.transpose(pb[:, i * 128:i * 128 + ntok], src_bf[0:ntok, cc * 128:(cc + 1) * 128],
                                           ident[0:ntok, 0:ntok])
                    return last
                S.op("pe", fn, reads=[b_src, b_id], writes=[bank[bk]])
                src3 = pb.rearrange("p (i n) -> p i n", i=8)[:, :, 0:ntok]
                dst3 = dstT[:, half * 8:half * 8 + 8, col0:col0 + ntok]
                if half == 0 and act_half0:
                    S.op("act", lambda a, s3=src3, d3=dst3: a.activation(out=d3, in_=s3, func=AF.Copy),
                         reads=[bank[bk]], uwrites=[b_dstT])
                else:
                    S.op("dve", lambda v, s3=src3, d3=dst3: v.tensor_copy(out=d3, in_=s3),
                         reads=[bank[bk]], uwrites=[b_dstT])

        def fe_norm(k, t):
            ntok = HALO if t == 0 else 128
            col0 = 0 if t == 0 else HALO + 128 * (t - 1)
            row0 = 1024 * k + col0
            xs = fe_cnt["n"] % 2
            S.dma("sp", lambda q: q.dma_start(out=xt[xs][0:ntok, :], in_=xw[row0:row0 + ntok, :]), b_xt[xs],
                  writes=[b_xt[xs]])
            rmsnorm_tile(xt[xs], ntok, ht[xs], ht[xs], b_xt[xs], b_ht[xs], b_ht[xs])
            return (k, xs, ntok, col0)

        def fe_tr(k, xs, ntok, col0):
            transpose_tile(ht[xs], ntok, hT[k % 2], col0, b_ht[xs], b_hT[k % 2], act_half0=False)

        def fe_tile(k, t):
            fe_tr(*fe_norm(k, t))

        pending_tr = []

        b_lx = [buf("lx0"), buf("lx1")]
        b_rr, b_ii, b_aa, b_xcb, b_gg = buf("rr"), buf("ii"), buf("aa"), buf("xcb"), buf("gg")
        b_xc2 = [b_xc, buf("xc_b")]
        b_rr2 = [b_rr, buf("rr_b")]
        b_xcb2 = [b_xcb, buf("xcb_b")]
        b_ii2 = [b_ii, buf("ii_b")]
        b_aa2 = [b_aa, buf("aa_b")]
        b_ylg, b_z = buf("ylg"), buf("z")
        lxc = {"n": 0, "h": 0}

        st_l = {}

        def lru_part1(k, c, job, colbase):
            hsl = k % 2
            slot = use_slab(job)
            wv = slab_view(slot, "K16")
            ls = lxc["n"] % 2
            lxc["n"] += 1
            q = c % 2
            setid = c % 2
            tok, b0, banks = mm_result(setid, lambda kk: wv[:, kk, colbase:colbase + 128],
                                       lambda kk, g: hT[hsl][:, kk, g * GW:(g + 1) * GW], KC,
                                       reads=[slab_buf[slot], b_hT[hsl]])
            if k < 3 and colbase == 128:
                release_slab(job)
            st_l[(k, c)] = (slot, wv, ls, b0, banks)

        def lru_part1_ew(k, c):
            lru_part1_ew_a(k, c)
            lru_part1_ew_b(k, c)

        def lru_part1_ew_a(k, c):
            slot, wv, ls, b0, banks = st_l[(k, c)]
            S.op("act", lambda a: a.activation(out=v3(lx[ls][:, :]), in_=ps3(b0), func=AF.Copy),
                 reads=banks, writes=[b_lx[ls]])

        def lru_part1_ew_b(k, c):
            slot, wv, ls, b0, banks = st_l[(k, c)]
            q = c % 2
            S.op("act", lambda a: a.activation(out=v3(xc2[q][:, :]), in_=ps3(b0), func=AF.Identity,
                                               scale=colp(P_CW4 + c * 4 + 3), bias=colp(P_CB + c)),
                 reads=banks + [b_par], writes=[b_xc2[q]])
            for tap in (2, 1, 0):
                sh = 3 - tap
                S.op("dve", lambda v, tap=tap, sh=sh: v.scalar_tensor_tensor(
                    out=xc2[q][:, 3:W], in0=lx[ls][:, 3 - sh:W - sh], scalar=colp(P_CW4 + c * 4 + tap), in1=xc2[q][:, 3:W],
                    op0=ALU.mult, op1=ALU.add), reads=[b_lx[ls], b_par], writes=[b_xc2[q]])
            S.op("dve", lambda v: v.tensor_copy(out=xcb2[q][:, :], in_=xc2[q][:, :]), reads=[b_xc2[q]], writes=[b_xcb2[q]])

        def lru_part2a(k, c, which_list=(0, 1)):
            q = c % 2
            rrq, b_rrq = rr2[q], b_rr2[q]
            for which in which_list:
                gw_ = gwa if which == 0 else gwx
                dst = rrq if which == 0 else ii2[q]
                bdst = b_rrq if which == 0 else b_ii2[q]
                pcol = (56 if which == 0 else 64) + c
                def gfn(pe, gw_=gw_):
                    last = None
                    for g in range(2):
                        last = pe.matmul(ps[:, 6 + g, :], lhsT=gw_[:, c, :], rhs=xcb2[q][:, HALO + 512 * g:HALO + 512 * (g + 1)],
                                         start=True, stop=True)
                    return last
                bks = [bank[6], bank[7]]
                S.op("pe", gfn, reads=[b_gw, b_xcb2[q]], writes=bks)
                S.op("act", lambda a, dst=dst, pcol=pcol: a.activation(
                    out=dst[:, HALO:W].rearrange("p (g n) -> p g n", g=2), in_=ps[:, 6:8, :], func=AF.Tanh,
                    bias=sm[:, pcol:pcol + 1], scale=0.5),
                    reads=bks + [b_sm], writes=[bdst])

        def lru_part2b(k, c, job):
            hsl = k % 2
            slot, wv, ls_, b0_, banks_ = st_l.pop((k, c))
            q = c % 2
            xcq, rrq, b_xcq, b_rrq = xc2[q], rr2[q], b_xc2[q], b_rr2[q]
            aq, iq, b_aq, b_iq = aa2[q], ii2[q], b_aa2[q], b_ii2[q]
            S.op("act", lambda a: a.activation(out=aq[:, HALO:W], in_=rrq[:, HALO:W], func=AF.Exp, scale=sm[:, 8 + c:9 + c], bias=sm[:, 8 + c:9 + c]),
                 reads=[b_rrq, b_sm], writes=[b_aq])
            S.op("act", lambda a: a.activation(out=rrq[:, HALO:W], in_=rrq[:, HALO:W], func=AF.Exp, scale=sm[:, c:c + 1], bias=sm[:, c:c + 1]),
                 reads=[b_sm], writes=[b_rrq])
            S.op("act", lambda a: a.activation(out=rrq[:, HALO:W], in_=rrq[:, HALO:W], func=AF.Sqrt, scale=-1.0, bias=1.0),
                 writes=[b_rrq])
            S.op("dve", lambda v: v.scalar_tensor_tensor(out=iq[:, HALO:W], in0=iq[:, HALO:W], scalar=1.0, in1=xcq[:, HALO:W], op0=ALU.add, op1=ALU.mult),
                 reads=[b_xcq], writes=[b_iq])
            S.op("dve", lambda v: v.scalar_tensor_tensor(out=iq[:, HALO:W], in0=iq[:, HALO:W], scalar=0.5, in1=rrq[:, HALO:W], op0=ALU.mult, op1=ALU.mult),
                 reads=[b_rrq], writes=[b_iq])
            hq = lxc["h"] % 2
            lxc["h"] += 1
            S.op("dve", lambda v: v.tensor_tensor_scan(out=hs[hq][:, HALO:W], data0=aq[:, HALO:W], data1=iq[:, HALO:W],
                                                       initial=sm[:, 16 + c:17 + c], op0=ALU.mult, op1=ALU.add),
                 reads=[b_aq, b_iq, b_sm], writes=[b_hs[hq]])
            if k < 3:
                if k == 2:
                    S.op("dve", lambda v: v.tensor_scalar(out=sm[:, 24 + 2 * c:26 + 2 * c], in0=hs[hq][:, W - 2:W],
                                                          scalar1=colp(P_MASK + 2), scalar2=None, op0=ALU.mult),
                         reads=[b_hs[hq], b_par], writes=[b_sm])
                S.op("dve", lambda v: v.tensor_scalar(out=sm[:, 16 + c:17 + c], in0=hs[hq][:, W - 1:W],
                                                      scalar1=colp(P_MASK + k), scalar2=None, op0=ALU.mult),
                     reads=[b_hs[hq], b_par], writes=[b_sm])
                return
            S.op("dve", lambda v: v.tensor_copy(out=hs[hq][:, HALO - 2:HALO], in_=sm[:, 24 + 2 * c:26 + 2 * c]),
                 reads=[b_sm], writes=[b_hs[hq]])
            sid = (c + 1) % 2
            tk, gb0, gbks = mm_result(sid, lambda kk: wv[:, kk, 128:256],
                                      lambda kk, g: hT[hsl][:, kk, g * GW:(g + 1) * GW], KC,
                                      reads=[slab_buf[slot], b_hT[hsl]])
            release_slab(job)
            S.op("act", lambda a: a.activation(out=v3(gg[:, :]), in_=ps3(gb0), func=AF.Copy), reads=gbks, writes=[b_gg])
            S.op("act", lambda a: a.activation(out=aq[:, :], in_=gg[:, :], func=AF.Square),
                 reads=[b_gg], writes=[b_aq])
            S.op("dve", lambda v: v.tensor_scalar(out=aq[:, :], in0=aq[:, :], scalar1=0.044715, scalar2=1.0,
                                                  op0=ALU.mult, op1=ALU.add), writes=[b_aq])
            S.op("dve", lambda v: v.tensor_tensor(out=aq[:, :], in0=aq[:, :], in1=gg[:, :], op=ALU.mult),
                 reads=[b_gg], writes=[b_aq])
            S.op("act", lambda a: a.activation(out=aq[:, :], in_=aq[:, :], func=AF.Tanh, scale=0.7978845608028654),
                 writes=[b_aq])
            S.op("dve", lambda v: v.scalar_tensor_tensor(out=gg[:, :], in0=aq[:, :], scalar=1.0, in1=gg[:, :], op0=ALU.add, op1=ALU.mult),
                 reads=[b_aq], writes=[b_gg])
            S.op("dve", lambda v: v.scalar_tensor_tensor(out=ylgT[:, c, :], in0=gg[:, :], scalar=0.5, in1=hs[hq][:, :], op0=ALU.mult, op1=ALU.mult),
                 reads=[b_gg, b_hs[hq]], uwrites=[b_ylg])
            if DEBUG:
                S.dma("sp", lambda q_: q_.dma_start(out=dbg["hs"][:, c * W:(c + 1) * W], in_=hs[hq][:, :]), buf("dbg_hs"),
                      reads=[b_hs[hq]])

        def fe_next(k, c):
            if k >= 3:
                return
            tiles = {0: (0, 1), 1: (2, 3), 2: (4, 5), 3: (6,), 4: (7,), 5: (8,)}.get(c, ())
            for t in tiles:
                pending_tr.append(fe_norm(k + 1, t))

        def job_of(k, c):
            if k < 3:
                return J_pre[k][c // 2], (c % 2) * 128
            return J_lx[c], 0

        S.epoch(b_hT[0])
        for t in range(NT + 1):
            fe_tile(0, t)
        seq = [(k, c) for k in range(4) for c in range(8)]
        j0, cb0 = job_of(0, 0)
        lru_part1(0, 0, j0, cb0)
        lru_part1_ew(0, 0)
        j1, cb1 = job_of(0, 1)
        lru_part1(0, 1, j1, cb1)
        for i_, (k, c) in enumerate(seq):
            if c == 0 and k < 3:
                S.epoch(b_hT[(k + 1) % 2])
            if k == 3 and c == 0:
                if DEBUG:
                    S.dma("sp", lambda q: q.dma_start(out=dbg["hT"][:, :], in_=hT[1][:, :, :].rearrange("p k n -> p (k n)")),
                          buf("dbg_hT"), reads=[b_hT[1]])
                S.takeover(b_ylg, [b_hT[0]])
                S.takeover(b_z, [b_hT[0]])
            nxt = seq[i_ + 1] if i_ + 1 < len(seq) else None
            nxt2 = seq[i_ + 2] if i_ + 2 < len(seq) else None
            lru_part2a(k, c, (0,))
            if nxt is not None:
                lru_part1_ew_a(*nxt)
            lru_part2a(k, c, (1,))
            if nxt is not None:
                lru_part1_ew_b(*nxt)
            while pending_tr:
                fe_tr(*pending_tr.pop(0))
            if nxt2 is not None:
                j2, cb2 = job_of(*nxt2)
                lru_part1(nxt2[0], nxt2[1], j2, cb2)
            lru_part2b(k, c, job_of(k, c)[0])
            fe_next(k, c)

        b_sbb, b_sbc, b_sbcv, b_sbacc = buf("sb_b"), buf("sb_c"), buf("sb_cv"), buf("sb_acc")
        lru_bufs = [b_lx[0], b_lx[1], b_xc, b_rr, b_ii, b_aa, b_hs[0], b_hs[1], b_gg, b_xcb, b_xc2[1], b_rr2[1], b_xcb2[1]]
        cbufs_extra = [b_ii2[1], b_aa2[1]]
        S.takeover(b_sbb, lru_bufs)
        S.takeover(b_sbc, lru_bufs)
        S.takeover(b_sbcv, lru_bufs)
        S.takeover(b_sbacc, lru_bufs)
        S.op("dve", lambda v: v.memset(sb_acc[:, 0:2], 0.0), writes=[b_sbacc])
        def sc_chunk(c):
            res = {}
            for wi, nm in enumerate(("b", "c", "v")):
                ch = 3 * c + wi
                job = J_sc[ch // 2]
                slot = use_slab(job)
                wv = slab_view(slot, "K16")
                cb_ = (ch % 2) * 128
                sid = next_set()
                tk, b0, bks = mm_result(sid, lambda kk, wv=wv, cb_=cb_: wv[:, kk, cb_:cb_ + 128],
                                        lambda kk, g: hT[1][:, kk, g * GW:(g + 1) * GW], KC,
                                        reads=[slab_buf[slot], b_hT[1]])
                if ch % 2 == 1:
                    release_slab(job)
                res[nm] = (b0, bks)
                if nm == "b":
                    S.op("act", lambda a, b0=b0: a.activation(out=v3(sb_b[:, :]), in_=ps3(b0), func=AF.Copy),
                         reads=bks, writes=[b_sbb])
                elif nm == "c":
                    S.op("act", lambda a, b0=b0: a.activation(out=v3(sb_c[:, :]), in_=ps3(b0), func=AF.Copy),
                         reads=bks, writes=[b_sbc])
                else:
                    S.op("dve", lambda v, b0=b0: v.tensor_tensor(out=v3(sb_cv[:, :]), in0=ps3(b0), in1=v3(sb_c[:, :]), op=ALU.mult),
                         reads=bks + [b_sbc], writes=[b_sbcv])
            S.op("act", lambda a: a.activation(out=sb_acc[:, 2:W], in_=sb_cv[:, 2:W], func=AF.Copy, scale=colp(P_SCW + c * 3 + 2)),
                 reads=[b_sbcv, b_par], writes=[b_sbacc])
            for tap in (1, 0):
                sh = 2 - tap
                S.op("dve", lambda v, tap=tap, sh=sh: v.scalar_tensor_tensor(
                    out=sb_acc[:, 2:W], in0=sb_cv[:, 2 - sh:W - sh], scalar=colp(P_SCW + c * 3 + tap), in1=sb_acc[:, 2:W],
                    op0=ALU.mult, op1=ALU.add), reads=[b_sbcv, b_par], writes=[b_sbacc])
            S.op("dve", lambda v: v.tensor_tensor(out=zT[:, c, :], in0=sb_acc[:, :], in1=sb_b[:, :], op=ALU.mult),
                 reads=[b_sbacc, b_sbb], uwrites=[b_z])
        for c_ in range(8):
            sc_chunk(c_)
        if DEBUG:
            S.dma("sp", lambda q: q.dma_start(out=dbg["z"][:, :], in_=zT[:, :, :].rearrange("p k n -> p (k n)")), buf("dbg_z"), reads=[b_z])
            S.dma("sp", lambda q: q.dma_start(out=dbg["ylg"][:, :], in_=ylgT[:, :, :].rearrange("p k n -> p (k n)")), buf("dbg_ylg"), reads=[b_ylg])

        b_sgl, b_tm1, b_merged = buf("sgl"), buf("tm1"), buf("merged")
        sc_bufs = [b_sbb, b_sbc, b_sbcv, b_sbacc]
        S.takeover(b_sgl, lru_bufs + sc_bufs)
        S.takeover(b_tm1, lru_bufs + sc_bufs)
        S.takeover(b_merged, [b_xt[0], b_xt[1], b_ht[0], b_ht[1]] + cbufs_extra)
        def gate_j(j):
            jp = j // 2
            jig = J_ig[j]
            jop = J_op[jp]
            s_ig = use_slab(jig)
            wig = slab_view(s_ig, "K16")
            if j % 2 == 0:
                s_op = use_slab(jop)
            else:
                s_op = jop % NSLOT
            wop = slab_view(s_op, "K8")
            for br in range(2):
                sid = next_set()
                tk, gb0, gbks = mm_result(sid, lambda kk, br=br: wig[:, kk, br * 128:(br + 1) * 128],
                                          lambda kk, g: hT[1][:, kk, g * GW:(g + 1) * GW], KC,
                                          reads=[slab_buf[s_ig], b_hT[1]])
                S.op("act", lambda a, gb0=gb0: a.activation(out=v3(sgl[:, :]), in_=ps3(gb0), func=AF.Sigmoid),
                     reads=gbks, writes=[b_sgl])
                src = ylgT if br == 0 else zT
                bsrc = b_ylg if br == 0 else b_z
                cb_ = ((j % 2) * 2 + br) * 128
                sid = next_set()
                tk, yb0, ybks = mm_result(sid, lambda kk, cb_=cb_: wop[:, kk, cb_:cb_ + 128],
                                          lambda kk, g, src=src: src[:, kk, g * GW:(g + 1) * GW], 8,
                                          reads=[slab_buf[s_op], bsrc])
                if br == 0:
                    S.op("dve", lambda v, yb0=yb0: v.tensor_tensor(out=v3(tm1[:, :]), in0=ps3(yb0), in1=v3(sgl[:, :]), op=ALU.mult),
                         reads=ybks + [b_sgl], writes=[b_tm1])
                else:
                    S.op("dve", lambda v, yb0=yb0: v.tensor_tensor(out=v3(sgl[:, :]), in0=ps3(yb0), in1=v3(sgl[:, :]), op=ALU.mult),
                         reads=ybks, writes=[b_sgl])
                    S.op("dve", lambda v: v.tensor_tensor(out=merged[:, j, :], in0=sgl[:, :], in1=tm1[:, :], op=ALU.add),
                         reads=[b_sgl, b_tm1], uwrites=[b_merged])
            release_slab(jig)
            if j % 2 == 1:
                release_slab(jop)
        for j_ in range(16):
            gate_j(j_)
        if DEBUG:
            S.dma("sp", lambda q: q.dma_start(out=dbg["merged"][:, :], in_=merged[:, :, :].rearrange("p k n -> p (k n)")), buf("dbg_m"), reads=[b_merged])

        b_xr = [buf("xr%d" % t) for t in range(NT)]
        b_xrh = buf("xrh")
        olds = [b_hT[0], b_hT[1], b_ylg, b_z]
        for t in range(NT):
            S.takeover(b_xr[t], olds)
        S.takeover(b_xrh, lru_bufs + sc_bufs + [b_sgl, b_tm1])
        for t in range(NT):
            r0 = 3 * 1024 + HALO + 128 * t
            S.dma("sp", lambda q, t=t, r0=r0: q.dma_start(out=xr[:, t, :], in_=xw[r0:r0 + 128, :]), b_xr[t], writes=[b_xr[t]])
        S.dma("sp", lambda q: q.dma_start(out=xrh[0:HALO, :], in_=xw[3 * 1024:3 * 1024 + HALO, :]), b_xrh, writes=[b_xrh])
        S.dma("sp", lambda q: q.dma_start(out=gB[:, :], in_=g3_d[1, :, :]), b_gB, writes=[b_gB])

        def wo_block(cbk):
            ja, jb = J_wo[2 * cbk], J_wo[2 * cbk + 1]
            sa = use_slab(ja)
            sb = use_slab(jb)
            wa_v, wb_v = slab_view(sa, "K16"), slab_view(sb, "K16")
            for t in range(NT + 1):
                ntok = HALO if t == 0 else 128
                col0 = 0 if t == 0 else HALO + 128 * (t - 1)
                bk = 2 * (rot["wo"] % 3)
                rot["wo"] += 1

                def fn(pe, ntok=ntok, col0=col0, bk=bk):
                    last = None
                    for kk in range(KC):
                        l_ = merged[:, kk, col0:col0 + ntok]
                        pe.matmul(ps[0:ntok, bk, 0:256], lhsT=l_, rhs=wa_v[:, kk, :], start=(kk == 0), stop=(kk == KC - 1))
                        last = pe.matmul(ps[0:ntok, bk + 1, 0:256], lhsT=l_, rhs=wb_v[:, kk, :], start=(kk == 0), stop=(kk == KC - 1))
                    return last
                S.op("pe", fn, reads=[b_merged, slab_buf[sa], slab_buf[sb]], writes=[bank[bk], bank[bk + 1]])
                if t == 0:
                    dst = xrh[0:HALO, cbk * 512:(cbk + 1) * 512]
                    bd = b_xrh
                else:
                    dst = xr[:, t - 1, cbk * 512:(cbk + 1) * 512]
                    bd = b_xr[t - 1]
                dst2 = dst.rearrange("p (b n) -> p b n", b=2)
                S.op("dve", lambda v, dst2=dst2, ntok=ntok, bk=bk: v.tensor_tensor(out=dst2, in0=ps[0:ntok, bk:bk + 2, 0:256], in1=dst2, op=ALU.add),
                     reads=[bank[bk], bank[bk + 1]], writes=[bd])
            release_slab(ja)
            release_slab(jb)
        for cbk_ in range(4):
            wo_block(cbk_)
        if DEBUG:
            S.dma("sp", lambda q: q.dma_start(out=dbg["xr"][:, :], in_=xr[:, :, :].rearrange("p k n -> p (k n)")), buf("dbg_xr"), reads=b_xr)

        b_h2t, b_sqj, b_h2T = [buf("h2t0"), buf("h2t1")], buf("sqj"), buf("h2T")
        S.takeover(b_h2T, [b_merged])
        for q_ in range(2):
            S.takeover(b_h2t[q_], lru_bufs + sc_bufs + [b_sgl, b_tm1])
        S.takeover(b_sqj, lru_bufs + sc_bufs + [b_sgl, b_tm1])
        for t in range(NT + 1):
            ntok = HALO if t == 0 else 128
            col0 = 0 if t == 0 else HALO + 128 * (t - 1)
            src = xrh if t == 0 else xr[:, t - 1, :]
            bsrc = b_xrh if t == 0 else b_xr[t - 1]
            q_ = t % 2
            rmsnorm_tile(src, ntok, h2t[q_], sqj, bsrc, b_h2t[q_], b_sqj)
            transpose_tile(h2t[q_], ntok, h2T, col0, b_h2t[q_], b_h2T)
        if DEBUG:
            S.dma("sp", lambda q: q.dma_start(out=dbg["h2T"][:, :], in_=h2T[:, :, :].rearrange("p k n -> p (k n)")), buf("dbg_h2T"), reads=[b_h2T])
        S.dma("sp", lambda q: q.dma_start(out=gB[:, :], in_=g3_d[2, :, :]), b_gB, writes=[b_gB])

        b_gated = [buf("gated0"), buf("gated1")]
        b_upg, b_upv, b_acg, b_acv, b_sgf = buf("upg"), buf("upv"), buf("acg"), buf("acv"), buf("sgf")
        old_d = lru_bufs + sc_bufs + [b_sgl, b_tm1, b_xrh, b_h2t[0], b_h2t[1], b_sqj]
        for bb_ in (b_gated[0], b_gated[1], b_upg, b_upv, b_acg, b_acv, b_sgf):
            S.takeover(bb_, old_d)

        def ffn_up(G):
            gs = G % 2
            for cl in range(FG):
                cch = G * FG + cl
                job = J_up[G][cl]
                slot = use_slab(job)
                wv = slab_view(slot, "K16")
                bb = {}
                for wi in range(2):
                    sid = next_set()
                    tk, b0, bks = mm_result(sid, lambda kk, wi=wi, wv=wv: wv[:, kk, wi * 128:(wi + 1) * 128],
                                            lambda kk, g: h2T[:, kk, g * GW:(g + 1) * GW], KC,
                                            reads=[slab_buf[slot], b_h2T])
                    raw = upg if wi == 0 else upv
                    acc = acg if wi == 0 else acv
                    braw = b_upg if wi == 0 else b_upv
                    bacc = b_acg if wi == 0 else b_acv
                    pc = P_FCW + ((cch if wi == 0 else NFC + cch)) * 3
                    S.op("act", lambda a, raw=raw, b0=b0: a.activation(out=v3(raw[:, :]), in_=ps3(b0), func=AF.Copy),
                         reads=bks, writes=[braw])
                    S.op("act", lambda a, acc=acc, b0=b0, pc=pc: a.activation(out=v3(acc[:, :]), in_=ps3(b0), func=AF.Copy,
                                                                              scale=colp(pc + 2)),
                         reads=bks + [b_par], writes=[bacc])
                    bb[wi] = (raw, acc, braw, bacc, pc)
                release_slab(job)
                for tap in (1, 0):
                    for wi in range(2):
                        raw, acc, braw, bacc, pc = bb[wi]
                        sh = 2 - tap
                        S.op("dve", lambda v, raw=raw, acc=acc, pc=pc, tap=tap, sh=sh: v.scalar_tensor_tensor(
                            out=acc[:, HALO:W], in0=raw[:, HALO - sh:W - sh], scalar=colp(pc + tap), in1=acc[:, HALO:W],
                            op0=ALU.mult, op1=ALU.add), reads=[braw, b_par], writes=[bacc])
                S.op("act", lambda a: a.activation(out=sgf[:, HALO:W], in_=acg[:, HALO:W], func=AF.Silu),
                     reads=[b_acg], writes=[b_sgf])
                S.op("dve", lambda v, gs=gs, cl=cl: v.tensor_tensor(out=gated[gs][:, cl, :], in0=sgf[:, HALO:W], in1=acv[:, HALO:W], op=ALU.mult),
                     reads=[b_sgf, b_acv], uwrites=[b_gated[gs]])

        dn_rot = {"n": 0}

        def ffn_down(G, final):
            gs = G % 2
            jd = J_dn[G]
            s0 = use_slab(jd[0])
            s1 = use_slab(jd[1])
            wd = [slab_view(s0, "DN"), slab_view(s1, "DN")]
            for tt in range(NT):
                for hlf in range(2):
                    pair = (3 + dn_rot["n"]) % 4
                    dn_rot["n"] += 1
                    d0 = 2 * pair
                    bks = bank[d0:d0 + 2]

                    def fn(pe, tt=tt, d0=d0, hlf=hlf):
                        last = None
                        for cl in range(FG):
                            l_ = gated[gs][:, cl, tt * 128:(tt + 1) * 128]
                            for nb in range(2):
                                c0 = (2 * hlf + nb) * 512
                                last = pe.matmul(ps[:, d0 + nb, :], lhsT=l_, rhs=wd[cl // 2][:, cl % 2, c0:c0 + 512],
                                                 start=(cl == 0), stop=(cl == FG - 1))
                        return last
                    S.op("pe", fn, reads=[b_gated[gs], slab_buf[s0], slab_buf[s1]], writes=bks)
                    S.op("dve", lambda v, tt=tt, d0=d0, hlf=hlf: v.tensor_tensor(
                        out=xr[:, tt, hlf * 1024:(hlf + 1) * 1024].rearrange("p (b n) -> p b n", b=2),
                        in0=ps[:, d0:d0 + 2, :],
                        in1=xr[:, tt, hlf * 1024:(hlf + 1) * 1024].rearrange("p (b n) -> p b n", b=2), op=ALU.add),
                        reads=bks, writes=[b_xr[tt]])
                if final:
                    final_tile(tt)
            release_slab(jd[0])
            release_slab(jd[1])

        b_ostg = [buf("ostg0"), buf("ostg1")]
        fin = {"init": False}

        def final_tile(tt):
            if not fin["init"]:
                fin["init"] = True
                S.takeover(b_ostg[0], [b_gated[0], b_gated[1]])
                S.takeover(b_ostg[1], [b_upg, b_upv, b_acg, b_acv, b_sgf])
                S.takeover(b_sqj, [b_gated[0], b_gated[1], b_upg, b_upv, b_acg, b_acv, b_sgf, b_sqj])
            q_ = tt % 2
            qq = fe_cnt["n"] % 2
            fe_cnt["n"] += 1
            ssc = sm[:, 40 + qq:41 + qq]
            ss2 = sm[:, 42 + qq:43 + qq]
            rs_ = sm[:, 44 + qq:45 + qq]
            bq = buf("ssq%d" % qq)
            S.op("act", lambda a: a.activation(out=sqj[:, :], in_=xr[:, tt, :], func=AF.Square, accum_out=ssc),
                 reads=[b_xr[tt]], writes=[b_sqj, bq])
            S.op("act", lambda a: a.activation(out=ss2, in_=ssc, func=AF.Copy), writes=[bq])
            S.op("act", lambda a: a.activation(out=rs_, in_=ss2, func=AF.Sqrt, scale=1.0 / D, bias=eps_ap),
                 reads=[b_eps], writes=[bq])
            S.op("dve", lambda v: v.reciprocal(out=rs_, in_=rs_), writes=[bq])
            S.op("dve", lambda v: v.scalar_tensor_tensor(out=ostg[q_][:, :], in0=xr[:, tt, :], scalar=rs_, in1=gB[:, :],
                                                        op0=ALU.mult, op1=ALU.mult),
                 reads=[b_xr[tt], bq, b_gB], writes=[b_ostg[q_]])
            tok = S.dma("sp", lambda q: q.dma_start(out=out_d[tt * 128:(tt + 1) * 128, :], in_=ostg[q_][:, :]), buf("ost%d" % q_),
                        reads=[b_ostg[q_]])
            out_toks.append(tok)

        out_toks = []
        for G in range(NGRP):
            ffn_up(G)
            if G >= 1:
                ffn_down(G - 1, False)
        ffn_down(NGRP - 1, True)
        S.wait_only("sp", out_toks)
        assert state_w["next_issue"] == len(jobs), (state_w["next_issue"], len(jobs))

        block = es.enter_context(nc.Block())
        S.replay(block)
    return nc


_CACHE = {}


def _perm_cols(w, order, width=128):
    return np.ascontiguousarray(np.concatenate([w[:, c * width:(c + 1) * width] for c in order], axis=1))


def kernel(x, g_mix, w_in, lru_conv_w, lru_conv_b, lru_wa, lru_ba, lru_wx, lru_bx, lru_lambda,
           lru_w_out, sc_conv_w, sc_w_out, w_o, g_ffn, ffn_w_up, ffn_conv_w, ffn_w_down, g_final):
    f = lambda a: np.ascontiguousarray(np.asarray(a, dtype=np.float32))
    x = f(x); w_in = f(w_in)
    Bsz, Sq, Dm = x.shape
    assert (Bsz, Sq, Dm) == (2, 4096, 2048)
    if "nc" not in _CACHE:
        _CACHE["nc"] = build_program()
    nc = _CACHE["nc"]

    order = []
    for c in range(8):
        order += [c, 8 + c]
    for c in range(8):
        order += [16 + c, 24 + c, 32 + c]
    for j in range(16):
        order += [40 + j, 56 + j]
    def slabs_k(wm, ncols):
        K_, N_ = wm.shape
        return np.ascontiguousarray(wm.reshape(K_ // 128, 128, N_ // ncols, ncols).transpose(2, 1, 0, 3).reshape(N_ // ncols, 128, -1))
    w_inp = slabs_k(_perm_cols(w_in, order), 256)
    w_lx = slabs_k(w_in[:, 0:1024], 256)
    lw, sw = f(lru_w_out), f(sc_w_out)
    w_oc = slabs_k(np.concatenate(
        [blk for j in range(16) for blk in (lw[:, j * 128:(j + 1) * 128], sw[:, j * 128:(j + 1) * 128])], axis=1), 512)
    wu = f(ffn_w_up)
    w_upp = slabs_k(np.concatenate(
        [blk for c in range(NFC) for blk in (wu[:, c * 128:(c + 1) * 128], wu[:, DFF + c * 128:DFF + (c + 1) * 128])], axis=1), 256)
    w_dn = np.ascontiguousarray(f(ffn_w_down).reshape(22, 2, 128, D).transpose(0, 2, 1, 3).reshape(22, 128, 4096))
    w_o_ = slabs_k(f(w_o), 256)

    def blkdiag(wg):
        wg = f(wg)
        o = np.zeros((128, 8, 128), np.float32)
        for c in range(8):
            o[0:64, c, 0:64] = wg[2 * c]
            o[64:128, c, 64:128] = wg[2 * c + 1]
        return o.reshape(128, 1024)
    gwa, gwx = blkdiag(lru_wa), blkdiag(lru_wx)

    def fm(v, n):
        return f(v).reshape(n, 128).T

    g3 = np.ascontiguousarray(np.stack([np.broadcast_to(f(g)[None, :], (128, D)) for g in (g_mix, g_ffn, g_final)]))
    ident = np.eye(128, dtype=np.float32)

    in_maps = []
    for core in range(8):
        b, j = core // 4, core % 4
        par = np.zeros((128, NPAR), np.float32)
        cw = f(lru_conv_w)
        par[:, P_CW4:P_CW4 + 32] = cw.reshape(4, 8, 128).transpose(2, 1, 0).reshape(128, 32)
        par[:, P_CB:P_CB + 8] = fm(lru_conv_b, 8)
        par[:, P_BA:P_BA + 8] = fm(lru_ba, 8)
        par[:, P_BX:P_BX + 8] = fm(lru_bx, 8)
        par[:, P_LAM:P_LAM + 8] = fm(lru_lambda, 8)
        sw3 = f(sc_conv_w)
        par[:, P_SCW:P_SCW + 24] = sw3.reshape(3, 8, 128).transpose(2, 1, 0).reshape(128, 24)
        for k in range(3):
            par[:, P_MASK + k] = 1.0 if (j + k - 3) >= 0 else 0.0
        par[:, P_MASK + 3] = 1.0
        fw = f(ffn_conv_w)
        par[:, P_FCW:P_FCW + 264] = fw.reshape(3, 88, 128).transpose(2, 1, 0).reshape(128, 264)
        xwin = np.zeros((4 * 1024 + HALO, D), np.float32)
        t0 = 1024 * j
        g_lo = t0 - 3072 - HALO
        lo = max(g_lo, 0)
        xwin[lo - g_lo:, :] = x[b, lo:t0 + 1024, :]
        in_maps.append({"xw": xwin, "w_lx": w_lx, "w_inp": w_inp, "w_oc": w_oc, "w_o": w_o_, "w_up": w_upp,
                        "w_dn": w_dn, "gwa": gwa, "gwx": gwx, "par": par, "g3": g3, "ident": ident})
    res = run_bass_kernel_spmd(nc, in_maps, core_ids=list(range(8)))
    _CACHE["res"] = res
    out = np.zeros((Bsz, Sq, Dm), np.float32)
    for core in range(8):
        b, j = core // 4, core % 4
        out[b, 1024 * j:1024 * (j + 1), :] = np.asarray(res.results[core]["out"], dtype=np.float32)
    return out
```
